# Optimizing a Trainium2 kernel written in Bass

```python
import jax, jax.numpy as jnp
from jax import lax
import numpy as np

D_MODEL = 1024
BATCH = 2
SEQ = 8192
DEPTH = 2

GRID_W = 64
CTX_LEN = 256
N_MIXERS = 2
NORM_EPS = 1e-6

MLSTM_HEADS = 8
MLSTM_DV = D_MODEL // MLSTM_HEADS
MLSTM_DQK = MLSTM_DV // 2
MLSTM_QK_W = MLSTM_HEADS * MLSTM_DQK
MLSTM_V_W = MLSTM_HEADS * MLSTM_DV
MLSTM_O_W = MLSTM_V_W
MLSTM_GATE_W = 4 * MLSTM_HEADS
MLSTM_IN_W = 2 * MLSTM_QK_W + MLSTM_V_W + MLSTM_O_W + MLSTM_GATE_W
MLSTM_SPLITS = [MLSTM_QK_W, 2 * MLSTM_QK_W, 2 * MLSTM_QK_W + MLSTM_V_W, 2 * MLSTM_QK_W + MLSTM_V_W + MLSTM_O_W]
MLSTM_CHUNK = 64
GATE_SOFTCAP = 15.0

POOL_WINDOWS = (2, 4, 8, 16)
POOL_GROUPS = len(POOL_WINDOWS)
POOL_GW = D_MODEL // POOL_GROUPS

D_FF = 4 * D_MODEL

kernel_name = "hybrid_mlstm_pool_flow_backbone"


def rmsnorm(x, g):
    xf = x.astype(jnp.float32)
    y = xf * lax.rsqrt(jnp.mean(xf * xf, axis=-1, keepdims=True) + NORM_EPS)
    return (y * g.astype(jnp.float32)).astype(x.dtype)


def mlp(u, w1, w2):
    h = jnp.square(jax.nn.relu(u @ w1))
    return h @ w2


def mlstm_project(u, w_in, gate_b):
    bsz, L = u.shape[0], u.shape[1]
    p = u @ w_in
    q, k, v, o, g = jnp.split(p, MLSTM_SPLITS, axis=-1)

    def heads(t, d):
        return t.reshape(bsz, L, MLSTM_HEADS, d).transpose(0, 2, 1, 3).astype(jnp.float32)

    q = heads(q, MLSTM_DQK) * (MLSTM_DQK ** -0.5)
    k = heads(k, MLSTM_DQK)
    v = heads(v, MLSTM_DV)
    g = g.reshape(bsz, L, 4, MLSTM_HEADS).astype(jnp.float32) + gate_b.astype(jnp.float32)
    g = GATE_SOFTCAP * jnp.tanh(g / GATE_SOFTCAP)
    g = g.transpose(2, 0, 3, 1)
    log_i = g[0::2]
    log_f = jax.nn.log_sigmoid(g[1::2])
    return q, k, v, o, log_i, log_f


def mlstm_zero_state(bsz):
    return (jnp.zeros((bsz, MLSTM_HEADS, MLSTM_DV, MLSTM_DQK), jnp.float32),
            jnp.zeros((bsz, MLSTM_HEADS, MLSTM_DQK), jnp.float32),
            jnp.zeros((bsz, MLSTM_HEADS), jnp.float32))


def mlstm_state_update(state, k, v, log_i, log_f):
    C, n, m = state
    b = jnp.cumsum(log_f, axis=-1)
    b_end = b[..., -1]
    g = b_end[..., None] - b + log_i
    m_new = jnp.maximum(b_end + m, jnp.max(g, axis=-1))
    a = jnp.exp(b_end + m - m_new)
    w = jnp.exp(g - m_new[..., None])
    C = a[..., None, None] * C + jnp.einsum('bhsv,bhsd->bhvd', w[..., None] * v, k)
    n = a[..., None] * n + jnp.einsum('bhs,bhsd->bhd', w, k)
    return (C, n, m_new)


def mlstm_chunk(state, chunk):
    q, k, v, log_i, log_f = chunk
    C, n, m = state
    L = q.shape[-2]
    b = jnp.cumsum(log_f, axis=-1)
    order = jnp.tril(jnp.ones((L, L), dtype=bool))
    dmat = jnp.where(order, b[..., :, None] - b[..., None, :] + log_i[..., None, :], -jnp.inf)
    inter = b + m[..., None]
    m_t = jnp.maximum(inter, jnp.max(dmat, axis=-1))
    w_inter = jnp.exp(inter - m_t)
    s = jnp.einsum('bhtd,bhsd->bhts', q, k) * jnp.exp(dmat - m_t[..., None])
    num = w_inter[..., None] * jnp.einsum('bhvd,bhtd->bhtv', C, q) + jnp.einsum('bhts,bhsv->bhtv', s, v)
    den = w_inter * jnp.einsum('bhd,bhtd->bht', n, q) + jnp.sum(s, axis=-1)
    h = num / jnp.maximum(jnp.abs(den), jnp.exp(-m_t))[..., None]
    return mlstm_state_update(state, k, v, log_i, log_f), h


def mlstm_scan(state, q, k, v, log_i, log_f):
    bsz, nh, L = q.shape[0], q.shape[1], q.shape[2]
    nc = L // MLSTM_CHUNK

    def to_chunks(t):
        return jnp.moveaxis(t.reshape((bsz, nh, nc, MLSTM_CHUNK) + t.shape[3:]), 2, 0)

    final, h = lax.scan(mlstm_chunk, state, (to_chunks(q), to_chunks(k), to_chunks(v), to_chunks(log_i), to_chunks(log_f)))
    return jnp.moveaxis(h, 0, 2).reshape(bsz, nh, L, MLSTM_DV), final


def mlstm_output(h, o, norm_g, w_out):
    bsz, nh, L, dv = h.shape
    h = h * lax.rsqrt(jnp.mean(h * h, axis=-1, keepdims=True) + NORM_EPS)
    h = h.transpose(0, 2, 1, 3).reshape(bsz, L, nh * dv) * norm_g.astype(jnp.float32)
    y = (jax.nn.sigmoid(o.astype(jnp.float32)) * h).astype(o.dtype)
    return y @ w_out


def mlstm_mixer(xn, cn, w_in, gate_b, norm_g, w_out, ctx_out):
    qx, kx, vx, ox, lix, lfx = mlstm_project(xn, w_in, gate_b)
    qc, kc, vc, oc, lic, lfc = mlstm_project(cn, w_in, gate_b)
    hx = 0.0
    hc = 0.0
    for d in range(2):
        if d == 0:
            order = lambda t: t
        else:
            order = lambda t: jnp.flip(t, axis=2)
        zero = mlstm_zero_state(kc.shape[0])
        c_k, c_v, c_li, c_lf = order(kc), order(vc), order(lic[d]), order(lfc[d])
        if ctx_out:
            h_c, state = mlstm_scan(zero, order(qc), c_k, c_v, c_li, c_lf)
            hc = hc + order(h_c)
        else:
            state = mlstm_state_update(zero, c_k, c_v, c_li, c_lf)
        h_x, _ = mlstm_scan(state, order(qx), order(kx), order(vx), order(lix[d]), order(lfx[d]))
        hx = hx + order(h_x)
    y = mlstm_output(hx, ox, norm_g, w_out)
    y_ctx = mlstm_output(hc, oc, norm_g, w_out) if ctx_out else None
    return y, y_ctx


def centred_mean(u, window):
    L = u.shape[-2]
    cs = jnp.cumsum(u, axis=-2)
    cs = jnp.concatenate([jnp.zeros_like(cs[..., :1, :]), cs], axis=-2)
    t = jnp.arange(L)
    lo = jnp.clip(t - window // 2, 0, L)
    hi = jnp.clip(t - window // 2 + window, 0, L)
    total = jnp.take(cs, hi, axis=-2) - jnp.take(cs, lo, axis=-2)
    return total / (hi - lo).astype(u.dtype)[:, None]


def pool_mixer(u, w, scale):
    uf = u.astype(jnp.float32)
    pooled = jnp.stack([centred_mean(uf[..., gi * POOL_GW:(gi + 1) * POOL_GW], win)
                        for gi, win in enumerate(POOL_WINDOWS)], axis=-2)
    p = pooled - uf.reshape(uf.shape[:-1] + (POOL_GROUPS, POOL_GW))
    y = jnp.einsum('...gc,gcd->...gd', p.astype(u.dtype), w)
    return y.reshape(u.shape) * scale


def setup_inputs(seed: int = 0) -> dict:
    key = jax.random.key(seed)
    ks = jax.random.split(key, 18)
    D = D_MODEL
    n_a = (DEPTH + 1) // N_MIXERS
    n_b = DEPTH // N_MIXERS

    def nrm(k, shape, s):
        return jax.random.normal(k, shape, jnp.float32) * s

    f_bias = jnp.linspace(3.0, 6.0, MLSTM_HEADS, dtype=jnp.float32)
    zeros_h = jnp.zeros((MLSTM_HEADS,), jnp.float32)
    gate_base = jnp.stack([zeros_h, f_bias, zeros_h, f_bias])
    return {
        "x": nrm(ks[0], (BATCH, SEQ, D), 1.0),
        "c": nrm(ks[1], (BATCH, D), 1.0),
        "ctx": nrm(ks[2], (BATCH, CTX_LEN, D), 1.0),
        "c_ctx": nrm(ks[3], (D,), 1.0),
        "ada_w": nrm(ks[4], (DEPTH, D, 6 * D), 0.5 * D ** -0.5),
        "ada_b": nrm(ks[5], (DEPTH, 6 * D), 0.02),
        "norm1_g": 1.0 + nrm(ks[6], (DEPTH, D), 0.02),
        "norm2_g": 1.0 + nrm(ks[7], (DEPTH, D), 0.02),
        "mlstm_w_in": nrm(ks[8], (n_a, D, MLSTM_IN_W), D ** -0.5),
        "mlstm_gate_b": gate_base + nrm(ks[9], (n_a, 4, MLSTM_HEADS), 0.1),
        "mlstm_norm_g": 1.0 + nrm(ks[10], (n_a, MLSTM_V_W), 0.02),
        "mlstm_w_out": nrm(ks[11], (n_a, MLSTM_V_W, D), MLSTM_V_W ** -0.5),
        "pool_w": nrm(ks[12], (n_b, POOL_GROUPS, POOL_GW, POOL_GW), POOL_GW ** -0.5),
        "pool_scale": 1.0 + nrm(ks[13], (n_b, D), 0.02),
        "mlp_w1": nrm(ks[14], (DEPTH, D, D_FF), D ** -0.5),
        "mlp_w2": nrm(ks[15], (DEPTH, D_FF, D), D_FF ** -0.5),
        "final_g": 1.0 + nrm(ks[16], (D,), 0.02),
    }


def reference(x, c, ctx, c_ctx, ada_w, ada_b, norm1_g, norm2_g, mlstm_w_in, mlstm_gate_b,
              mlstm_norm_g, mlstm_w_out, pool_w, pool_scale, mlp_w1, mlp_w2, final_g):
    bsz, seq, dm = x.shape
    rows = seq // GRID_W
    silu_c = jax.nn.silu(c)
    silu_cc = jax.nn.silu(c_ctx)
    for i in range(DEPTH):
        kind = i % N_MIXERS
        j = i // N_MIXERS
        ctx_next = any(l % N_MIXERS == 0 for l in range(i + 1, DEPTH))
        mod = (silu_c @ ada_w[i] + ada_b[i])[:, None, :]
        sh1, sc1, g1, sh2, sc2, g2 = jnp.split(mod, 6, axis=-1)
        xn = rmsnorm(x, norm1_g[i]) * (1 + sc1) + sh1
        if kind == 0 or ctx_next:
            cmod = silu_cc @ ada_w[i] + ada_b[i]
            csh1, csc1, cg1, csh2, csc2, cg2 = jnp.split(cmod, 6)
            cn = rmsnorm(ctx, norm1_g[i]) * (1 + csc1) + csh1
        if kind == 0:
            y, y_ctx = mlstm_mixer(xn, cn, mlstm_w_in[j], mlstm_gate_b[j], mlstm_norm_g[j], mlstm_w_out[j], ctx_next)
        else:
            y = pool_mixer(xn.reshape(bsz, rows, GRID_W, dm), pool_w[j], pool_scale[j]).reshape(bsz, seq, dm)
            y_ctx = pool_mixer(cn, pool_w[j], pool_scale[j]) if ctx_next else None
        x = x + g1 * y
        x = x + g2 * mlp(rmsnorm(x, norm2_g[i]) * (1 + sc2) + sh2, mlp_w1[i], mlp_w2[i])
        if ctx_next:
            ctx = ctx + cg1 * y_ctx
            ctx = ctx + cg2 * mlp(rmsnorm(ctx, norm2_g[i]) * (1 + csc2) + csh2, mlp_w1[i], mlp_w2[i])
    return rmsnorm(x, final_g)
```

```python
import contextlib
import numpy as np
import ml_dtypes
import concourse.bass as bass
import concourse.mybir as mybir
from concourse.bass_utils import run_bass_kernel_spmd

F32 = mybir.dt.float32
BF16 = mybir.dt.bfloat16
AF = mybir.ActivationFunctionType
ALU = mybir.AluOpType
AX = mybir.AxisListType

D = 1024
NT = 16
TOK = 2048
H = 8
DQK = 64
DV = 128
DVA = 129
INW = 3104
DFF = 4096
EPS = 1e-6
CAP = 15.0
NEG = -30000.0
SUMW = 1032 + 8


class Buf:
    __slots__ = ("name", "w", "r", "excl")

    def __init__(self, name, excl=False):
        self.name = name
        self.excl = excl
        self.w = None
        self.r = []


class Eng:
    def __init__(self, name, h, sem):
        self.name, self.h, self.sem = name, h, sem
        self.count = 0
        self.known = {}


class Prog:
    def __init__(self, nc, es):
        self.nc = nc
        self.es = es
        self.sems = {}
        self.E = {}
        for name, h in (("pe", nc.tensor), ("act", nc.scalar), ("dve", nc.vector),
                        ("pool", nc.gpsimd), ("sp", nc.sync)):
            sem = es.enter_context(nc.semaphore("s_" + name))
            self.sems["s_" + name] = sem
            self.E[name] = Eng(name, h, sem)
        self.dq = {}
        for q, nq in (("sp", 24), ("pool", 12)):
            lst = []
            for i in range(nq):
                nm = "d_%s%d" % (q, i)
                sem = es.enter_context(nc.semaphore(nm))
                self.sems[nm] = sem
                lst.append([nm, 0])
            self.dq[q] = [lst, 0]

    def _need(self, eng, ev, hazard):
        if ev is None:
            return
        key, val, owner = ev
        if owner == eng.name and hazard != "raw":
            return
        if owner is not None and owner != eng.name:
            assert val <= self.E[owner].count, "wait on un-incremented event %s %d" % (owner, val)
        if eng.known.get(key, 0) >= val:
            return
        eng.h.wait_ge(self.sems[key], val)
        eng.known[key] = val

    def _deps(self, eng, reads, writes):
        for b in reads:
            self._need(eng, b.w, "raw")
            if b.excl:
                for ev in b.r:
                    if ev[2] != eng.name:
                        self._need(eng, ev, "raw")
        for b in writes:
            self._need(eng, b.w, "waw")
            for ev in b.r:
                self._need(eng, ev, "war")

    def _commit(self, ev, reads, writes):
        for b in writes:
            b.w = ev
            b.r = []
        for b in reads:
            if b in writes:
                continue
            b.r = [e for e in b.r if e[0] != ev[0]] + [ev]

    def op(self, engname, fn, reads=(), writes=()):
        eng = self.E[engname]
        self._deps(eng, reads, writes)
        ins = fn(eng.h)
        eng.count += 1
        ins.then_inc(eng.sem, 1)
        ev = ("s_" + engname, eng.count, engname)
        self._commit(ev, reads, writes)

    def mm(self, fn, reads=(), writes=(), last=True):
        eng = self.E["pe"]
        self._deps(eng, reads, writes)
        ins = fn(eng.h)
        ev = ("s_pe", eng.count + 1, "pe")
        if last:
            eng.count += 1
            ins.then_inc(eng.sem, 1)
        self._commit(ev, reads, writes)

    def dma(self, q, out, in_, reads=(), writes=()):
        eng = self.E[q]
        self._deps(eng, reads, writes)
        lst, idx = self.dq[q]
        ent = lst[idx % len(lst)]
        self.dq[q][1] = idx + 1
        if ent[1] > 0 and eng.known.get(ent[0], 0) < ent[1]:
            eng.h.wait_ge(self.sems[ent[0]], ent[1])
            eng.known[ent[0]] = ent[1]
        ent[1] += 16
        eng.h.dma_start(out=out, in_=in_).then_inc(self.sems[ent[0]], 16)
        ev = (ent[0], ent[1], None)
        self._commit(ev, reads, writes)
        return ev

    def wait_event(self, engname, ev):
        self._need(self.E[engname], ev, "raw")

    def barrier_all(self, engs=("pe", "act", "dve", "pool", "sp")):
        for a in engs:
            ea = self.E[a]
            for b in engs:
                if a == b:
                    continue
                eb = self.E[b]
                if eb.count > 0 and ea.known.get("s_" + b, 0) < eb.count:
                    ea.h.wait_ge(eb.sem, eb.count)
                    ea.known["s_" + b] = eb.count


def _pool_mats():
    out = np.zeros((4, 128, 128), np.float32)
    for gi, win in enumerate((2, 4, 8, 16)):
        pm = np.zeros((128, 128), np.float32)
        for r in range(2):
            for t in range(64):
                lo = min(max(t - win // 2, 0), 64)
                hi = min(max(t - win // 2 + win, 0), 64)
                pm[r * 64 + t, r * 64 + lo:r * 64 + hi] = 1.0 / (hi - lo)
        out[gi] = (pm - np.eye(128, dtype=np.float32)).T
    return out


def _consts():
    c = {}
    c["ident"] = np.eye(128, dtype=np.float32)
    s = np.arange(128)[:, None]
    t = np.arange(128)[None, :]
    c["utri"] = (s <= t).astype(np.float32)
    c["ltri"] = (s >= t).astype(np.float32)
    c["ones"] = np.ones((128, 128), np.float32)
    negf = np.where(s <= t, 0.0, NEG).astype(np.float32)
    negb = np.where(s >= t, 0.0, NEG).astype(np.float32)
    c["negm"] = np.ascontiguousarray(np.broadcast_to(np.stack([negf, negb], 1)[:, :, None, :], (128, 2, 8, 128)))
    c["pmt"] = np.ascontiguousarray(_pool_mats().transpose(1, 0, 2))
    return c


def build(mode="fused", stop=None):
    nc = bass.Bass("TRN2", target_bir_lowering=False)

    def din(name, shape, dt=F32):
        return nc.dram_tensor(name, list(shape), dt, kind="ExternalInput").ap()

    x_d = din("x", [TOK, D])
    ctx_d = din("ctx", [256, D])
    cT_d = din("cT", [128, 8, 2])
    adaw_d = din("ada_w", [2, D, 6 * D])
    adab_d = din("ada_b", [2, 6 * D])
    n1g_d = din("norm1_g", [2, D])
    n2g_d = din("norm2_g", [2, D])
    fing_d = din("final_g", [1, D])
    mng_d = din("mlstm_norm_g", [1, D])
    psc_d = din("pool_scale", [1, D])
    gb_d = din("gate_b", [1, 32])
    win_d = din("w_in", [D, INW])
    wout_d = din("w_out", [D, D])
    pw_d = din("pool_w", [4, 256, 256])
    w1_d = din("w1", [2, D, DFF])
    w2_d = din("w2", [2, DFF, D])
    ident_d = din("ident", [128, 128])
    utri_d = din("utri", [128, 128])
    ltri_d = din("ltri", [128, 128])
    ones_d = din("ones", [128, 128])
    negm_d = din("negm", [128, 2, 8, 128])
    pmt_d = din("pmt", [128, 4, 128])
    mk_d = din("segmask", [128, 3, 4])
    xoth_d = din("x_oth", [3, TOK, D])
    out_d = nc.dram_tensor("out", [TOK, D], F32, kind="ExternalOutput").ap()
    RECW = 512 + 512 + 512 + H * DVA + 1024
    rec_d = nc.dram_tensor("rec", [NT, 128, RECW], BF16, kind="Internal").ap()

    _cnt = [0]

    def un(name):
        _cnt[0] += 1
        return "sb%d_%s" % (_cnt[0], name)

    es = contextlib.ExitStack()
    with es:
        P = Prog(nc, es)

        def sb(name, shape, dt=F32):
            return es.enter_context(nc.sbuf_tensor(un(name), list(shape), dt))

        PS = es.enter_context(nc.psum_tensor("ps", [128, 8, 512], F32))
        PB = [Buf("psb%d" % i, excl=True) for i in range(8)]

        bX = [Buf("X%d" % i) for i in range(NT)]
        xs_d = nc.dram_tensor("xs_scr", [NT, 128, D], F32, kind="Internal").ap()
        bXS = [Buf("XS%d" % i) for i in range(NT)]
        identf = sb("identf", [128, 128]); identb = sb("identb", [128, 128], BF16)
        utri = sb("utri", [128, 128]); ltri = sb("ltri", [128, 128]); ones = sb("ones", [128, 128])
        bC = Buf("consts")
        MOD = sb("MOD", [128, 3, D]); bMOD = [Buf("MOD%d" % i) for i in range(3)]
        vrep = sb("vrep", [128, D]); bvrep = Buf("vrep")
        stats = [sb("stat%d" % i, [128, 8]) for i in range(2)]; bstats = [Buf("stat0"), Buf("stat1")]
        stat, bstat = stats[0], bstats[0]
        nzs = [sb("nz%d" % i, [128, D]) for i in range(2)]; bnzs = [Buf("nz0"), Buf("nz1")]
        zt = sb("zt", [128, D]); bzt = Buf("zt")
        xnbs = [sb("xnb%d" % i, [128, D], BF16) for i in range(2)]; bxnbs = [Buf("xnb0"), Buf("xnb1")]
        nrm_ = [0]
        cTs = sb("cTs", [128, 8, 2]); bcT = Buf("cTs")
        screp = sb("screp", [128, 2, 8, 128]); bscrep = Buf("screp")

        for dst, src in ((identf, ident_d), (utri, utri_d), (ltri, ltri_d), (ones, ones_d)):
            P.dma("sp", dst[:], src[:, :], writes=[bC])
        P.dma("pool", identb[:], ident_d[:, :], writes=[bC])
        P.dma("sp", cTs[:], cT_d[:, :, :], writes=[bcT])

        P.op("act", lambda e: e.activation(out=cTs[:], in_=cTs[:], func=AF.Silu), reads=[bcT], writes=[bcT])
        for j in range(2):
            for kc in range(8):
                P.op("dve", lambda e, j=j, kc=kc: e.tensor_scalar(
                    out=screp[:, j, kc, :], in0=ones[:], scalar1=cTs[:, kc, j:j + 1], scalar2=None,
                    op0=ALU.mult), reads=[bcT, bC], writes=[bscrep])

        def compute_mods(layer, blocks, dst, dbufs, cidx, stk, second=None):
            astg = [stk.enter_context(nc.sbuf_tensor(un("astg%d" % i), [128, 8, 512], F32)) for i in range(2)]
            bast = [Buf("astg%d" % i) for i in range(2)]
            abr = stk.enter_context(nc.sbuf_tensor(un("abr"), [128, 2, 512], F32)); babr = [Buf("abr0"), Buf("abr1")]
            n = 0
            for i, blk in enumerate(blocks):
                for hf in range(2):
                    c0 = blk * D + hf * 512
                    sl = n % 2
                    P.dma("sp", astg[sl][:], adaw_d[layer].rearrange("(kc p) n -> p kc n", p=128)[:, :, c0:c0 + 512],
                          writes=[bast[sl]])
                    P.dma("sp", abr[:, sl, :], pbc(adab_d[layer:layer + 1, c0:c0 + 512]),
                          writes=[babr[sl]])
                    bank = 6 + sl
                    for kc in range(8):
                        P.mm(lambda e, kc=kc, sl=sl, bank=bank: e.matmul(
                            PS[:, bank, 0:512], lhsT=screp[:, cidx, kc, :], rhs=astg[sl][:, kc, :],
                            start=(kc == 0), stop=(kc == 7)),
                            reads=[bscrep, bast[sl]], writes=[PB[bank]], last=(kc == 7))
                    P.op("dve", lambda e, i=i, hf=hf, sl=sl, bank=bank: e.tensor_tensor(
                        out=dst[:, i, hf * 512:(hf + 1) * 512], in0=PS[:, bank, 0:512], in1=abr[:, sl, :], op=ALU.add),
                        reads=[PB[bank], babr[sl]], writes=[dbufs[i]])
                    if second is not None and i < second[3]:
                        dst2, dbufs2, cidx2 = second[0], second[1], second[2]
                        bank2 = 4 + sl
                        for kc in range(8):
                            P.mm(lambda e, kc=kc, sl=sl, bank2=bank2: e.matmul(
                                PS[:, bank2, 0:512], lhsT=screp[:, cidx2, kc, :], rhs=astg[sl][:, kc, :],
                                start=(kc == 0), stop=(kc == 7)),
                                reads=[bscrep, bast[sl]], writes=[PB[bank2]], last=(kc == 7))
                        P.op("dve", lambda e, i=i, hf=hf, sl=sl, bank2=bank2: e.tensor_tensor(
                            out=dst2[:, i, hf * 512:(hf + 1) * 512], in0=PS[:, bank2, 0:512], in1=abr[:, sl, :], op=ALU.add),
                            reads=[PB[bank2], babr[sl]], writes=[dbufs2[i]])
                    n += 1

        def pbc(row_ap):
            a = row_ap.partition_broadcast(128)
            if len(a.shape) == 3:
                a = a.rearrange("p o n -> p (o n)")
            return a

        def load_vrep(src_row_ap):
            P.dma("sp", vrep[:], pbc(src_row_ap), writes=[bvrep])

        def make_GS(gain_row_ap, M=None, bM=None):
            M = MOD if M is None else M
            bM = bMOD if bM is None else bM
            load_vrep(gain_row_ap)
            P.op("dve", lambda e: e.scalar_tensor_tensor(
                out=M[:, 1, :], in0=M[:, 1, :], scalar=1.0, in1=vrep[:], op0=ALU.add, op1=ALU.mult),
                reads=[bM[1], bvrep], writes=[bM[1]])

        def norm_tile(src_ap, src_bufs, gs, gsbuf):
            i = nrm_[0] % 2; nrm_[0] += 1
            st_, bst_, nz_, bnz_, xo_, bxo_ = stats[i], bstats[i], nzs[i], bnzs[i], xnbs[i], bxnbs[i]
            P.op("act", lambda e: e.activation(out=nz_[:], in_=src_ap, func=AF.Square, accum_out=st_[:, 0:1]),
                 reads=list(src_bufs), writes=[bnz_, bst_])
            P.op("act", lambda e: e.activation(out=st_[:, 1:2], in_=st_[:, 0:1], func=AF.Sqrt, scale=1.0 / D, bias=EPS),
                 reads=[bst_], writes=[bst_])
            P.op("dve", lambda e: e.reciprocal(out=st_[:, 2:3], in_=st_[:, 1:2]), reads=[bst_], writes=[bst_])
            P.op("dve", lambda e: e.scalar_tensor_tensor(out=nz_[:], in0=src_ap, scalar=st_[:, 2:3], in1=gs[:, 1, :],
                                                         op0=ALU.mult, op1=ALU.mult),
                 reads=list(src_bufs) + [bst_] + list(gsbuf), writes=[bnz_])
            P.op("dve", lambda e: e.tensor_tensor(out=xo_[:], in0=nz_[:], in1=gs[:, 0, :], op=ALU.add),
                 reads=[bnz_] + list(gsbuf), writes=[bxo_])
            return xo_, bxo_

        PSB = [PS[:, i, :].bitcast(BF16) for i in range(8)]

        def transpose_tile(src_bf, src_buf, dstT_ap_fn, dst_buf, bank):
            for kc in range(8):
                P.mm(lambda e, kc=kc: e.transpose(out=PSB[bank][:, kc * 128:(kc + 1) * 128],
                                                  in_=src_bf[:, kc * 128:(kc + 1) * 128], identity=identb[:]),
                     reads=[src_buf, bC], writes=[PB[bank]], last=(kc == 7))
            P.op("act", lambda e: e.activation(out=dstT_ap_fn(), in_=PSB[bank].rearrange("p (k t) -> p k t", k=8),
                                               func=AF.Copy), reads=[PB[bank]], writes=[dst_buf])

        l0 = contextlib.ExitStack()
        with l0:
            def sb0(name, shape, dt=F32):
                return l0.enter_context(nc.sbuf_tensor(un(name), list(shape), dt))
            MODC = sb0("MODC", [128, 2, D]); bMODC = [Buf("MODC0"), Buf("MODC1")]

            NTA = NT + 2
            gbrep = sb0("gbrep", [128, 32]); bgb = Buf("gbrep")
            P.dma("sp", gbrep[:], pbc(gb_d[0:1, :]), writes=[bgb])
            GRAW = sb0("GRAW", [128, NTA, 32]); bGRAW = Buf("GRAW")
            CK = sb0("CK", [128, 2, 512], BF16); CV = sb0("CVa", [128, 2, H, DVA], BF16); bCKV = Buf("CKV")
            P.op("pool", lambda e: e.memset(CV[:, :, :, 128:129], 1.0), writes=[bCKV])
            bRECD = [Buf("recd%d" % i) for i in range(NT)]
            bRECQ = [Buf("recq%d" % i) for i in range(NT)]
            OQ, OK_, OKT, OV, OSO = 0, 512, 1024, 1536, 1536 + H * DVA
            TMW = RECW - OKT

            SALL = sb0("SALL", [128, 3, SUMW]); bSALL = Buf("SALL")
            GC = sb0("GC", [128, NTA, 4, 8]); E1 = sb0("E1", [128, NTA, 4, 8]); bG = Buf("gates")
            LF = sb0("LF", [128, NTA, 2, 8]); Bc = sb0("Bc", [128, NTA, 2, 8]); BT = sb0("BT", [128, NTA, 2, 8])
            CVc = sb0("CVc", [128, NTA, 2, 8]); WALL = sb0("WALL", [128, NTA, 2, 8]); FT = sb0("FT", [128, NTA, 2, 8])
            FS = sb0("FS", [128, NTA, 2, 4]); BTX = sb0("BTX", [128, NT, 2, 8]); PFS = sb0("PFS", [128, NT, 2, 4])
            FSEG = sb0("FSEG", [128, 2, 4]); BSUM = sb0("BSUM", [128, 2, 8])
            WSG = sb0("WSG", [128, NT, 2, 8])
            ST = sb0("ST", [128, 2, 4, DVA]); bST = Buf("ST")
            STC = sb0("STC", [128, 2, 4, DVA]); bSTC = Buf("STC")
            WVt = [sb0("WVt%d" % i, [128, H, DVA], BF16) for i in range(2)]; bWV = [Buf("WV0"), Buf("WV1")]
            KVb = [sb0("KVb%d" % i, [128, 512 + H * DVA], BF16) for i in range(4)]; bKV = [Buf("KVb%d" % i) for i in range(4)]
            bCS = [Buf("CS%d" % i) for i in range(NT)]

            def gates_pass(nta):
                graw4 = GRAW[:, 0:nta].rearrange("p t (y h) -> p t y h", y=4)
                P.op("act", lambda e: e.activation(out=GC[:, 0:nta], in_=graw4, func=AF.Tanh, scale=1.0 / CAP), reads=[bGRAW], writes=[bG])
                P.op("act", lambda e: e.activation(out=E1[:, 0:nta], in_=GC[:, 0:nta], func=AF.Exp, scale=-CAP), reads=[bG], writes=[bG])
                P.op("dve", lambda e: e.tensor_scalar(out=E1[:, 0:nta], in0=E1[:, 0:nta], scalar1=1.0, scalar2=None, op0=ALU.add), reads=[bG], writes=[bG])
                P.op("act", lambda e: e.activation(out=E1[:, 0:nta], in_=E1[:, 0:nta], func=AF.Ln), reads=[bG], writes=[bG])
                for d in range(2):
                    P.op("dve", lambda e, d=d: e.tensor_scalar(out=LF[:, 0:nta, d, :], in0=E1[:, 0:nta, 2 * d + 1, :], scalar1=-1.0,
                                                               scalar2=None, op0=ALU.mult), reads=[bG], writes=[bG])
                GPS = PS[:, 0, :].rearrange("p (t x d h) -> p t x d h", t=16, x=2, d=2)
                for t0, ntl in (((0, 16), (16, 2)) if nta > 16 else ((0, 16),)):
                    for tt in range(ntl):
                        j = t0 + tt
                        P.mm(lambda e, j=j, tt=tt: e.matmul(GPS[:, tt, 0, 0, :], lhsT=utri[:], rhs=LF[:, j, 0, :], start=True, stop=True),
                             reads=[bG, bC], writes=[PB[0]], last=False)
                        P.mm(lambda e, j=j, tt=tt: e.matmul(GPS[:, tt, 0, 1, :], lhsT=ltri[:], rhs=LF[:, j, 1, :], start=True, stop=True),
                             reads=[bG, bC], writes=[PB[0]], last=False)
                        P.mm(lambda e, j=j, tt=tt: e.matmul(GPS[:, tt, 1, :, :], lhsT=ones[:], rhs=LF[:, j, :, :], start=True, stop=True),
                             reads=[bG, bC], writes=[PB[0]], last=(tt == ntl - 1))
                    P.op("dve", lambda e, t0=t0, ntl=ntl: e.tensor_copy(out=Bc[:, t0:t0 + ntl], in_=GPS[:, 0:ntl, 0]),
                         reads=[PB[0]], writes=[bG])
                    P.op("dve", lambda e, t0=t0, ntl=ntl: e.tensor_copy(out=BT[:, t0:t0 + ntl], in_=GPS[:, 0:ntl, 1]),
                         reads=[PB[0]], writes=[bG])
                for d in range(2):
                    P.op("dve", lambda e, d=d: e.scalar_tensor_tensor(
                        out=CVc[:, 0:nta, d, :], in0=GC[:, 0:nta, 2 * d, :], scalar=CAP, in1=Bc[:, 0:nta, d, :], op0=ALU.mult, op1=ALU.subtract),
                        reads=[bG], writes=[bG])
                P.op("dve", lambda e: e.tensor_tensor(out=WALL[:, 0:nta], in0=BT[:, 0:nta], in1=CVc[:, 0:nta], op=ALU.add), reads=[bG], writes=[bG])
                P.op("act", lambda e: e.activation(out=WALL[:, 0:nta], in_=WALL[:, 0:nta], func=AF.Exp), reads=[bG], writes=[bG])
                P.op("act", lambda e: e.activation(out=FT[:, 0:nta], in_=BT[:, 0:nta], func=AF.Exp), reads=[bG], writes=[bG])
                for hl in range(2):
                    FTv = FT[:, 0:nta].rearrange("p t d (hp l) -> p t d hp l", l=2)
                    P.op("dve", lambda e, hl=hl, FTv=FTv: e.tensor_copy(out=FS[hl * 64:(hl + 1) * 64, 0:nta], in_=FTv[hl * 64:(hl + 1) * 64, :, :, :, hl]),
                         reads=[bG], writes=[bG])
                P.op("dve", lambda e: e.memset(BTX[:, 0, 0, :], 0.0), writes=[bG])
                P.op("dve", lambda e: e.memset(BTX[:, NT - 1, 1, :], 0.0), writes=[bG])
                for j in range(1, NT):
                    P.op("dve", lambda e, j=j: e.tensor_tensor(out=BTX[:, j, 0, :], in0=BTX[:, j - 1, 0, :], in1=BT[:, j - 1, 0, :], op=ALU.add),
                         reads=[bG], writes=[bG])
                    jb = NT - 1 - j
                    P.op("dve", lambda e, jb=jb: e.tensor_tensor(out=BTX[:, jb, 1, :], in0=BTX[:, jb + 1, 1, :], in1=BT[:, jb + 1, 1, :], op=ALU.add),
                         reads=[bG], writes=[bG])
                P.op("dve", lambda e: e.tensor_tensor(out=BSUM[:, 0, :], in0=BTX[:, NT - 1, 0, :], in1=BT[:, NT - 1, 0, :], op=ALU.add), reads=[bG], writes=[bG])
                P.op("dve", lambda e: e.tensor_tensor(out=BSUM[:, 1, :], in0=BTX[:, 0, 1, :], in1=BT[:, 0, 1, :], op=ALU.add), reads=[bG], writes=[bG])
                P.op("dve", lambda e: e.tensor_tensor(out=WSG[:], in0=CVc[:, 0:NT], in1=BTX[:], op=ALU.subtract), reads=[bG], writes=[bG])
                P.op("dve", lambda e: e.tensor_tensor(out=WSG[:], in0=WSG[:], in1=BSUM[:].unsqueeze(1).to_broadcast([128, NT, 2, 8]), op=ALU.add),
                     reads=[bG], writes=[bG])
                P.op("act", lambda e: e.activation(out=WSG[:], in_=WSG[:], func=AF.Exp), reads=[bG], writes=[bG])
                P.op("act", lambda e: e.activation(out=BTX[:], in_=BTX[:], func=AF.Exp), reads=[bG], writes=[bG])
                P.op("act", lambda e: e.activation(out=BSUM[:], in_=BSUM[:], func=AF.Exp), reads=[bG], writes=[bG])
                for hl in range(2):
                    Xv = BTX[:].rearrange("p t d (hp l) -> p t d hp l", l=2)
                    Sv = BSUM[:].rearrange("p d (hp l) -> p d hp l", l=2)
                    P.op("dve", lambda e, hl=hl, Xv=Xv: e.tensor_copy(out=PFS[hl * 64:(hl + 1) * 64], in_=Xv[hl * 64:(hl + 1) * 64, :, :, :, hl]),
                         reads=[bG], writes=[bG])
                    P.op("dve", lambda e, hl=hl, Sv=Sv: e.tensor_copy(out=FSEG[hl * 64:(hl + 1) * 64], in_=Sv[hl * 64:(hl + 1) * 64, :, :, hl]),
                         reads=[bG], writes=[bG])

            def st_view(t, d):
                return t[:, d].rearrange("p (a b) c -> p a (b c)", a=2)

            def ss_wv(args, slot):
                kf, vf, bkf, jf, kb, vb, bkb, jb = args
                for d, (vt, bk, j) in enumerate(((vf, bkf, jf), (vb, bkb, jb))):
                    w_ = WVt[2 * slot + d]
                    P.op("dve", lambda e, d=d, vt=vt, j=j, w_=w_: e.tensor_tensor(
                        out=w_[:], in0=vt, in1=WALL[:, j, d, :].unsqueeze(2).to_broadcast([128, H, DVA]), op=ALU.mult),
                        reads=[bk, bG], writes=[bWV[2 * slot + d]])

            def ss_mm(args, slot):
                kf, vf, bkf, jf, kb, vb, bkb, jb = args
                for d, (kt, bk) in enumerate(((kf, bkf), (kb, bkb))):
                    w_ = WVt[2 * slot + d]
                    for h in range(H):
                        hp, hl = h // 2, h % 2
                        bank = 4 + 2 * d + hp // 2
                        c0 = (hp % 2) * DVA
                        P.mm(lambda e, kt=kt, h=h, hl=hl, bank=bank, c0=c0, w_=w_: e.matmul(
                            PS[hl * 64:(hl + 1) * 64, bank, c0:c0 + DVA], lhsT=kt[:, h * 64:(h + 1) * 64], rhs=w_[:, h, :],
                            start=True, stop=True), reads=[bk, bWV[2 * slot + d]], writes=[PB[bank]], last=(h == H - 1))

            def ss_st(args):
                kf, vf, bkf, jf, kb, vb, bkb, jb = args
                for d, j in ((0, jf), (1, jb)):
                    P.op("dve", lambda e, d=d, j=j: e.tensor_tensor(
                        out=ST[:, d], in0=ST[:, d], in1=FS[:, j, d, :].unsqueeze(2).to_broadcast([128, 4, DVA]), op=ALU.mult),
                        reads=[bST, bG], writes=[bST])
                    P.op("dve", lambda e, d=d: e.tensor_tensor(
                        out=st_view(ST, d), in0=st_view(ST, d), in1=PS[:, 4 + 2 * d:6 + 2 * d, 0:2 * DVA], op=ALU.add),
                        reads=[bST, PB[4 + 2 * d], PB[5 + 2 * d]], writes=[bST])

            def state_step(kf, vf, bkf, jf, kb, vb, bkb, jb):
                args = (kf, vf, bkf, jf, kb, vb, bkb, jb)
                ss_wv(args, 0); ss_mm(args, 0); ss_st(args)

            def load_kv(j, slot):
                P.dma("sp", KVb[slot][:], rec_d[j][:, OKT:OKT + 512 + H * DVA], reads=[bRECD[j]], writes=[bKV[slot]])

            def kv_views(slot):
                t = KVb[slot]
                return t[:, 0:512], t[:, 512:].rearrange("p (h c) -> p h c", c=DVA)

            def b1_pass(cs_tile):
                def step_args(stp):
                    jf, jb = stp, NT - 1 - stp
                    sf, sbk = (2 * stp) % 4, (2 * stp + 1) % 4
                    kf, vf = kv_views(sf); kb, vb = kv_views(sbk)
                    return (kf, vf, bKV[sf], jf, kb, vb, bKV[sbk], jb)
                load_kv(0, 0); load_kv(NT - 1, 1)
                load_kv(1, 2); load_kv(NT - 2, 3)
                ss_wv(step_args(0), 0)
                for stp in range(NT):
                    jf, jb = stp, NT - 1 - stp
                    args = step_args(stp)
                    if cs_tile is not None:
                        P.op("act", lambda e, jf=jf: e.activation(out=cs_tile[:, jf, 0], in_=ST[:, 0], func=AF.Copy), reads=[bST], writes=[bCS[jf]])
                        P.op("act", lambda e, jb=jb: e.activation(out=cs_tile[:, jb, 1], in_=ST[:, 1], func=AF.Copy), reads=[bST], writes=[bCS[jb]])
                    ss_mm(args, stp % 2)
                    if stp + 1 < NT:
                        ss_wv(step_args(stp + 1), (stp + 1) % 2)
                    ss_st(args)
                    if stp + 2 < NT:
                        load_kv(stp + 2, (2 * stp) % 4); load_kv(NT - 3 - stp, (2 * stp + 1) % 4)

            MK = sb0("MK", [128, 3, 4]); bMK = Buf("MK")
            P.dma("sp", MK[:], mk_d[:, :, :], writes=[bMK])
            WE = sb0("WE", [128, NT, 8]); bWE = Buf("WE")

            def b1_lite(k):
                P.op("dve", lambda e: e.tensor_scalar(out=WE[:], in0=WSG[:, :, 0, :], scalar1=MK[:, k, 0:1], scalar2=None, op0=ALU.mult),
                     reads=[bG, bMK], writes=[bWE])
                P.op("dve", lambda e: e.scalar_tensor_tensor(out=WE[:], in0=WSG[:, :, 1, :], scalar=MK[:, k, 2:3], in1=WE[:],
                                                             op0=ALU.mult, op1=ALU.add), reads=[bG, bMK, bWE], writes=[bWE])
                for j0 in range(3):
                    load_kv(j0, j0)
                for j in range(NT):
                    if j + 3 < NT:
                        load_kv(j + 3, (j + 3) % 4)
                    kt, vt = kv_views(j % 4)
                    bk = bKV[j % 4]
                    wslot = j % 2
                    P.op("dve", lambda e, vt=vt, j=j, wslot=wslot: e.tensor_tensor(
                        out=WVt[wslot][:], in0=vt, in1=WE[:, j, :].unsqueeze(2).to_broadcast([128, H, DVA]), op=ALU.mult),
                        reads=[bk, bWE], writes=[bWV[wslot]])
                    for h in range(H):
                        hp, hl = h // 2, h % 2
                        P.mm(lambda e, kt=kt, h=h, hl=hl, hp=hp, j=j, wslot=wslot: e.matmul(
                            PS[hl * 64:(hl + 1) * 64, hp, 0:DVA], lhsT=kt[:, h * 64:(h + 1) * 64], rhs=WVt[wslot][:, h, :],
                            start=(j == 0), stop=(j == NT - 1)), reads=[bk, bWV[wslot]], writes=[PB[hp]],
                            last=(h == H - 1))
                P.op("dve", lambda e: e.tensor_copy(out=ST[:, 0], in_=PS[:, 0:4, 0:DVA]), reads=PB[0:4], writes=[bST])

            p1 = contextlib.ExitStack()
            with p1:
                def sbp(name, shape, dt=F32):
                    return p1.enter_context(nc.sbuf_tensor(un(name), list(shape), dt))

                WIN = sbp("WIN", [128, 8, INW], BF16); bWIN = Buf("WIN")
                wv = win_d.rearrange("(kc p) n -> p kc n", p=128)
                for c0 in range(0, INW, 512):
                    c1 = min(INW, c0 + 512)
                    P.dma("pool", WIN[:, :, c0:c1], wv[:, :, c0:c1], writes=[bWIN])
                with contextlib.ExitStack() as stk:
                    compute_mods(0, [0, 1, 2], MOD, bMOD, 0, stk, second=(MODC, bMODC, 1, 2))
                    P.barrier_all()
                make_GS(n1g_d[0:1, :])

                def proj_tokmajor(xT, bxT, tcol, c0, n, bank):
                    for kc in range(8):
                        P.mm(lambda e, kc=kc: e.matmul(PS[:, bank, 0:n], lhsT=xT[:, kc, tcol:tcol + 128],
                                                       rhs=WIN[:, kc, c0:c0 + n], start=(kc == 0), stop=(kc == 7)),
                             reads=[bxT, bWIN], writes=[PB[bank]], last=(kc == 7))

                with contextlib.ExitStack() as cstk:
                    make_GS(n1g_d[0:1, :], MODC, bMODC)
                    ctile = cstk.enter_context(nc.sbuf_tensor(un("ctile"), [128, D], F32)); bct = Buf("ctile")
                    cxT = cstk.enter_context(nc.sbuf_tensor(un("cxT"), [128, 8, 256], BF16)); bcxT = Buf("cxT")
                    for ct in range(2):
                        P.dma("sp", ctile[:], ctx_d[ct * 128:(ct + 1) * 128, :], writes=[bct])
                        xn_, bxn_ = norm_tile(ctile[:], [bct], MODC, bMODC)
                        transpose_tile(xn_, bxn_, lambda ct=ct: cxT[:, :, ct * 128:(ct + 1) * 128], bcxT, 7)
                    for ct in range(2):
                        proj_tokmajor(cxT, bcxT, ct * 128, 512, 512, 2)
                        P.op("act", lambda e, ct=ct: e.activation(out=CK[:, ct, :], in_=PS[:, 2, :], func=AF.Copy),
                             reads=[PB[2]], writes=[bCKV])
                        proj_tokmajor(cxT, bcxT, ct * 128, 1024, 512, 3)
                        proj_tokmajor(cxT, bcxT, ct * 128, 1536, 512, 4)
                        P.op("dve", lambda e, ct=ct: e.tensor_copy(
                            out=CV[:, ct, :, 0:128], in_=PS[:, 3:5, :].rearrange("p b (h c) -> p (b h) c", c=128)),
                            reads=[PB[3], PB[4]], writes=[bCKV])
                        proj_tokmajor(cxT, bcxT, ct * 128, 3072, 32, 7)
                        P.op("dve", lambda e, ct=ct: e.tensor_tensor(out=GRAW[:, NT + ct, :], in0=PS[:, 7, 0:32], in1=gbrep[:], op=ALU.add),
                             reads=[PB[7], bgb], writes=[bGRAW])
                    P.barrier_all()

                load_vrep(mng_d[0:1, :])
                xnT = [sbp("xnT%d" % i, [128, 8, 512], BF16) for i in range(2)]
                bxnT = [Buf("xnT0"), Buf("xnT1")]
                QKs = sbp("QKs", [128, 2, 4, 512], BF16); bQK = Buf("QKs")
                RECs = [sbp("recs%d" % i, [128, TMW], BF16) for i in range(2)]
                bREC = [Buf("recs%d" % i) for i in range(2)]
                xst = [sbp("xst%d" % i, [128, D]) for i in range(2)]; bxst = [Buf("xst0"), Buf("xst1")]

                def rec_v(r):
                    return r[:, OV - OKT:OV - OKT + H * DVA].rearrange("p (h c) -> p h c", c=DVA)

                for i, r in enumerate(RECs):
                    P.op("pool", lambda e, r=r: e.memset(rec_v(r)[:, :, 128:129], 1.0), writes=[bREC[i]])
                nrec_ = [0]

                def p1_load(xsrc, j):
                    P.dma("sp", xst[j % 2][:], xsrc[j * 128:(j + 1) * 128, :], writes=[bxst[j % 2]])

                def p1_qk(g):
                    xT, bxT = xnT[g % 2], bxnT[g % 2]
                    for which, c0 in ((0, 0), (1, 512)):
                        for hp in range(4):
                            bank = hp % 2
                            for kc in range(8):
                                P.mm(lambda e, kc=kc, hp=hp, c0=c0, bank=bank: e.matmul(
                                    PS[:, bank, :], lhsT=WIN[:, kc, c0 + hp * 128:c0 + (hp + 1) * 128],
                                    rhs=xT[:, kc, :], start=(kc == 0), stop=(kc == 7)),
                                    reads=[bxT, bWIN], writes=[PB[bank]], last=(kc == 7))
                            P.op("act", lambda e, hp=hp, bank=bank, which=which: e.activation(
                                out=QKs[:, which, hp, :], in_=PS[:, bank, :], func=AF.Copy,
                                scale=(DQK ** -0.5 if which == 0 else 1.0)),
                                reads=[PB[bank]], writes=[bQK])
                    for tl in range(4):
                        j = g * 4 + tl
                        P.dma("sp", rec_d[j][:, 0:1024].rearrange("p (w a t) -> p w a t", w=2, a=4),
                              QKs[:, :, :, tl * 128:(tl + 1) * 128], reads=[bQK], writes=[bRECQ[j]])

                def p1_proj(j, lite):
                    g, tl = j // 4, j % 4
                    xT, bxT = xnT[g % 2], bxnT[g % 2]
                    ri = j % 2
                    r = RECs[ri]
                    proj_tokmajor(xT, bxT, tl * 128, 512, 512, 2)
                    P.op("act", lambda e, r=r: e.activation(out=r[:, 0:512], in_=PS[:, 2, :], func=AF.Copy),
                         reads=[PB[2]], writes=[bREC[ri]])
                    proj_tokmajor(xT, bxT, tl * 128, 1024, 512, 3)
                    proj_tokmajor(xT, bxT, tl * 128, 1536, 512, 4)
                    P.op("dve", lambda e, r=r: e.tensor_copy(
                        out=rec_v(r)[:, :, 0:128], in_=PS[:, 3:5, :].rearrange("p b (h c) -> p (b h) c", c=128)),
                        reads=[PB[3], PB[4]], writes=[bREC[ri]])
                    if not lite:
                        proj_tokmajor(xT, bxT, tl * 128, 2048, 512, 5)
                        proj_tokmajor(xT, bxT, tl * 128, 2560, 512, 6)
                        P.op("act", lambda e: e.activation(out=zt[:].rearrange("p (b c) -> p b c", b=2), in_=PS[:, 5:7, :],
                                                           func=AF.Sigmoid), reads=[PB[5], PB[6]], writes=[bzt])
                        P.op("dve", lambda e, r=r: e.tensor_tensor(out=r[:, OSO - OKT:OSO - OKT + D], in0=zt[:], in1=vrep[:], op=ALU.mult),
                             reads=[bzt, bvrep], writes=[bREC[ri]])
                    proj_tokmajor(xT, bxT, tl * 128, 3072, 32, 1)
                    P.op("dve", lambda e, j=j: e.tensor_tensor(out=GRAW[:, j, :], in0=PS[:, 1, 0:32], in1=gbrep[:], op=ALU.add),
                         reads=[PB[1], bgb], writes=[bGRAW])

                def p1_store(j, lite):
                    ri = j % 2
                    r = RECs[ri]
                    if lite:
                        P.dma("sp", rec_d[j][:, OKT:OSO], r[:, 0:OSO - OKT], reads=[bREC[ri]], writes=[bRECD[j]])
                    else:
                        P.dma("sp", rec_d[j][:, OKT:RECW], r[:], reads=[bREC[ri]], writes=[bRECD[j]])

                pend_ = {}

                def p1_steps(xsrc, lite, i0, i1):
                    if i0 == 0:
                        pend_.clear()
                        p1_load(xsrc, 0)
                        p1_load(xsrc, 1)
                        pend_[0] = norm_tile(xst[0][:], [bxst[0]], MOD, bMOD)
                    for i in range(i0, i1):
                        if i + 2 < NT:
                            p1_load(xsrc, i + 2)
                        if i + 1 < NT:
                            pend_[i + 1] = norm_tile(xst[(i + 1) % 2][:], [bxst[(i + 1) % 2]], MOD, bMOD)
                        t = i - 4
                        if 0 <= t < NT:
                            if t % 4 == 0 and not lite:
                                p1_qk(t // 4)
                            p1_proj(t, lite)
                        if 0 <= t - 1 < NT:
                            p1_store(t - 1, lite)
                        if i < NT:
                            xn_, bxn_ = pend_.pop(i)
                            xT, bxT = xnT[(i // 4) % 2], bxnT[(i // 4) % 2]
                            transpose_tile(xn_, bxn_, lambda tl=i % 4, xT=xT: xT[:, :, tl * 128:(tl + 1) * 128], bxT, 7)

                p1_steps(xoth_d[0], True, 0, 4)
                for k in range(3):
                    p1_steps(xoth_d[k], True, 4, NT + 5)
                    if k + 1 < 3:
                        p1_steps(xoth_d[k + 1], True, 0, 4)
                    else:
                        p1_steps(x_d, False, 0, 4)
                    gates_pass(NT)
                    b1_lite(k)
                    for d in range(2):
                        P.op("dve", lambda e, k=k, d=d: e.tensor_copy(out=SALL[:, k, 516 * d:516 * (d + 1)], in_=ST[:, 0].rearrange("p a c -> p (a c)")),
                             reads=[bST], writes=[bSALL])
                    P.op("dve", lambda e, k=k: e.tensor_copy(out=SALL[:, k, 1032:1040], in_=FSEG[:].rearrange("p d a -> p (d a)")), reads=[bG], writes=[bSALL])
                p1_steps(x_d, False, 4, NT + 5)
                P.barrier_all()

            gates_pass(NTA)
            CS = sb0("CS", [128, NT, 2, 4, DVA], BF16)
            modc_bf = MODC[:].rearrange("p a n -> p (a n)").bitcast(BF16)
            for i_ in range(2, 4):
                WVt.append(modc_bf[:, (i_ - 2) * H * DVA:(i_ - 1) * H * DVA].rearrange("p (h c) -> p h c", c=DVA)); bWV.append(Buf("WV%d" % i_))
            P.op("dve", lambda e: e.memset(ST[:], 0.0), writes=[bST])
            ckv = lambda ct: (CK[:, ct, :], CV[:, ct])
            for stp in range(2):
                cf, cb = stp, 1 - stp
                state_step(ckv(cf)[0], ckv(cf)[1], bCKV, NT + cf, ckv(cb)[0], ckv(cb)[1], bCKV, NT + cb)
            P.op("dve", lambda e: e.tensor_copy(out=STC[:], in_=ST[:]), reads=[bST], writes=[bSTC])
            P.op("dve", lambda e: e.memset(ST[:], 0.0), reads=[bSTC], writes=[bST])
            b1_pass(CS)

            cms = contextlib.ExitStack()

            def sbc(name, shape, dt=F32):
                return cms.enter_context(nc.sbuf_tensor(un(name), list(shape), dt))
            fe = sbc("fe", [128, 4]); bfe = Buf("fe")
            for d in range(2):
                for k in (range(3) if d == 0 else range(2, -1, -1)):
                    fsrc = SALL[:, k, 1032 + 4 * d:1032 + 4 * d + 4]
                    dsrc = SALL[:, k, 516 * d:516 * (d + 1)]
                    P.op("dve", lambda e, fsrc=fsrc, k=k, d=d: e.tensor_scalar(
                        out=fe[:], in0=fsrc, scalar1=MK[:, k, 2 * d:2 * d + 1], scalar2=MK[:, k, 2 * d + 1:2 * d + 2],
                        op0=ALU.mult, op1=ALU.add), reads=[bSALL, bMK], writes=[bfe])
                    P.op("dve", lambda e, d=d: e.tensor_tensor(
                        out=STC[:, d], in0=STC[:, d], in1=fe[:].unsqueeze(2).to_broadcast([128, 4, DVA]), op=ALU.mult),
                        reads=[bSTC, bfe], writes=[bSTC])
                    P.op("dve", lambda e, d=d, dsrc=dsrc, k=k: e.scalar_tensor_tensor(
                        out=STC[:, d].rearrange("p a c -> p (a c)"), in0=dsrc, scalar=MK[:, k, 2 * d:2 * d + 1],
                        in1=STC[:, d].rearrange("p a c -> p (a c)"), op0=ALU.mult, op1=ALU.add),
                        reads=[bSALL, bMK, bSTC], writes=[bSTC])
            tmpS = sbc("tmpS", [128, 2, 4, DVA]); btmpS = Buf("tmpS")
            for j in range(NT):
                P.op("dve", lambda e, j=j: e.tensor_tensor(
                    out=tmpS[:].rearrange("p d a c -> p (d a) c"), in0=STC[:].rearrange("p d a c -> p (d a) c"),
                    in1=PFS[:, j].rearrange("p d a -> p (d a)").unsqueeze(2).to_broadcast([128, 8, DVA]), op=ALU.mult),
                    reads=[bSTC, bG], writes=[btmpS])
                P.op("pool", lambda e, j=j: e.tensor_tensor(
                    out=CS[:, j].rearrange("p d a c -> p (d a c)"), in0=CS[:, j].rearrange("p d a c -> p (d a c)"),
                    in1=tmpS[:].rearrange("p d a c -> p (d a c)"), op=ALU.add), reads=[btmpS, bCS[j]], writes=[bCS[j]])

            P.barrier_all()
            cms.close()
            xsb = [sb0("xsb%d" % i, [128, D]) for i in range(2)]; bxsb = [Buf("xsb0"), Buf("xsb1")]
            negm = sb0("negm", [128, 2, 8, 128], BF16); bNEG = Buf("negm")
            P.dma("pool", negm[:], negm_d[:, :, :, :], writes=[bNEG])
            WO = sb0("WO", [128, 8, D], BF16); bWO = Buf("WO")
            P.dma("pool", WO[:], wout_d.rearrange("(kc p) n -> p kc n", p=128), writes=[bWO])
            RB = [sb0("RB%d" % i, [128, RECW], BF16) for i in range(2)]; bRB = [Buf("RB0"), Buf("RB1")]
            TM0 = sb0("TM", [128, H, 128]); EB0 = sb0("EB", [128, H, 128])
            TMs = [TM0, nzs[0][:].rearrange("p (h t) -> p h t", t=128)]; bTMs = [Buf("TM0"), Buf("TM1")]
            EBs = [EB0, nzs[1][:].rearrange("p (h t) -> p h t", t=128)]; bEBs = [Buf("EB0"), Buf("EB1")]
            DT_ = sb0("DT", [128, 2, H, 128], BF16); bDT = [Buf("DT0"), Buf("DT1")]
            sall_bf = SALL[:].rearrange("p k w -> p (k w)").bitcast(BF16)
            PT0 = sb0("PT", [128, 2, H, 128], BF16)
            QS0 = sb0("QS", [128, 2, 4, 128], BF16)
            PT1 = sall_bf[:, 0:2048].rearrange("p (d h t) -> p d h t", d=2, h=H)
            QS1 = sall_bf[:, 2048:3072].rearrange("p (d a t) -> p d a t", d=2, a=4)
            PTs = [PT0, PT1]; QSs = [QS0, QS1]
            bPTs = [[Buf("PT%d_%d" % (i, d)) for d in range(2)] for i in range(2)]
            bQSs = [[Buf("QS%d_%d" % (i, d)) for d in range(2)] for i in range(2)]
            HX = MODC[:, 0, :].rearrange("p (h c) -> p h c", c=128); bHX = Buf("HX")
            HT = MODC[:, 1, :].rearrange("p (h c) -> p h c", c=128); bHT = Buf("HT")
            dn = sb0("dn", [128, 4, H]); bdn = Buf("dn")
            yb = sb0("yb", [128, D], BF16); byb = Buf("yb")
            yT = sb0("yT", [128, 8, 128], BF16); byT = Buf("yT")

            def load_rec(j):
                P.dma("sp", RB[j % 2][:], rec_d[j], reads=[bRECD[j], bRECQ[j]], writes=[bRB[j % 2]])

            def rec_views(j):
                R = RB[j % 2]
                return (R[:, OQ:OQ + 512].rearrange("p (a t) -> p a t", a=4), R[:, OK_:OK_ + 512].rearrange("p (a t) -> p a t", a=4),
                        R[:, OV:OV + H * DVA].rearrange("p (h c) -> p h c", c=DVA), R[:, OSO:OSO + D], bRB[j % 2])

            def a_z(j, d):
                TM, bTM = TMs[d], bTMs[d]
                tri = utri if d == 0 else ltri
                P.op("dve", lambda e: e.tensor_tensor(
                    out=TM[:].rearrange("p (l a) t -> p l a t", l=2),
                    in0=tri[:].unsqueeze(1).unsqueeze(1).to_broadcast([128, 2, 4, 128]),
                    in1=LF[:, j, d, :].rearrange("p (a l) -> p l a", l=2).unsqueeze(3).to_broadcast([128, 2, 4, 128]),
                    op=ALU.mult), reads=[bC, bG], writes=[bTM])

            def a_brow(j, d):
                TM, bTM = TMs[d], bTMs[d]
                for bank in range(2):
                    P.mm(lambda e, bank=bank: e.matmul(
                        PS[:, bank, :], lhsT=ones[:], rhs=TM[:, 4 * bank:4 * bank + 4, :], start=True, stop=True),
                        reads=[bC, bTM], writes=[PB[bank]], last=(bank == 1))

            def a_elem(j, d):
                TM, bTM, EB, bEB = TMs[d], bTMs[d], EBs[d], bEBs[d]
                qT, kT, va, so, bR = rec_views(j)
                QS, bQS = QSs[j % 2], bQSs[j % 2]
                brow = PS[:, 0:2, :].rearrange("p b (h t) -> p (b h) t", t=128)
                P.op("act", lambda e: e.activation(out=EB[:], in_=brow, func=AF.Exp), reads=[PB[0], PB[1]], writes=[bEB])
                P.op("dve", lambda e: e.tensor_tensor(out=TM[:], in0=brow, in1=negm[:, d], op=ALU.add),
                     reads=[PB[0], PB[1], bNEG], writes=[bTM])
                for h in range(H):
                    sl = (h % 2) * 4 + h // 2
                    P.op("act", lambda e, h=h, sl=sl: e.activation(
                        out=DT_[:, d, sl, :], in_=TM[:, sl, :], func=AF.Exp, bias=CVc[:, j, d, h:h + 1]),
                        reads=[bTM, bG], writes=[bDT[d]])
                EBv = EB[:].rearrange("p (l a) t -> p l a t", l=2)
                for hl in range(2):
                    P.op("dve", lambda e, hl=hl: e.tensor_tensor(
                        out=QS[hl * 64:(hl + 1) * 64, d], in0=qT[hl * 64:(hl + 1) * 64],
                        in1=EBv[hl * 64:(hl + 1) * 64, hl, :, :], op=ALU.mult),
                        reads=[bR, bEB], writes=[bQS[d]])

            def a_s(j):
                qT, kT, va, so, bR = rec_views(j)
                PT_, bPT = PTs[j % 2], bPTs[j % 2]
                for hl in range(2):
                    for hp in range(4):
                        bank = 2 + hl
                        P.mm(lambda e, hp=hp, hl=hl, bank=bank: e.matmul(
                            PS[:, bank, hp * 128:(hp + 1) * 128], lhsT=kT[hl * 64:(hl + 1) * 64, hp, :],
                            rhs=qT[hl * 64:(hl + 1) * 64, hp, :], start=True, stop=True),
                            reads=[bR], writes=[PB[bank]], last=(hp == 3))
                srow = PS[:, 2:4, :].rearrange("p b (h t) -> p (b h) t", t=128)
                for d in range(2):
                    P.op("dve", lambda e, d=d: e.tensor_tensor(out=PT_[:, d], in0=srow, in1=DT_[:, d], op=ALU.mult),
                         reads=[PB[2], PB[3], bDT[d]], writes=[bPT[d]])

            Hv = PS[:, 4:8, 0:2 * DVA].rearrange("p b (a c) -> p b a c", c=DVA)
            hb = [PB[4], PB[5], PB[6], PB[7]]

            def b_h(j, d):
                qT, kT, va, so, bR = rec_views(j)
                PT_, QS, bPT, bQS = PTs[j % 2], QSs[j % 2], bPTs[j % 2], bQSs[j % 2]
                for hl in range(2):
                    for hp in range(4):
                        h = 2 * hp + hl
                        sl = hl * 4 + hp
                        bank = 4 + sl // 2
                        c0 = (sl % 2) * DVA
                        P.mm(lambda e, h=h, sl=sl, bank=bank, c0=c0: e.matmul(
                            PS[:, bank, c0:c0 + DVA], lhsT=PT_[:, d, sl, :], rhs=va[:, h, :], start=True, stop=False),
                            reads=[bPT[d], bR], writes=[PB[bank]], last=False)
                        P.mm(lambda e, hp=hp, hl=hl, bank=bank, c0=c0: e.matmul(
                            PS[:, bank, c0:c0 + DVA], lhsT=QS[hl * 64:(hl + 1) * 64, d, hp, :],
                            rhs=CS[hl * 64:(hl + 1) * 64, j, d, hp, :], start=False, stop=True),
                            reads=[bQS[d], bCS[j]], writes=[PB[bank]], last=(sl % 2 == 1))

            def b_evac(j, d):
                dnv = dn[:, d].rearrange("p (b a) -> p b a", a=2)
                P.op("dve", lambda e: e.tensor_copy(out=dnv, in_=Hv[:, :, :, 128]), reads=hb, writes=[bdn])
                P.op("dve", lambda e: e.scalar_tensor_tensor(out=dn[:, 2 + d], in0=dn[:, d], scalar=-1.0, in1=dn[:, d],
                                                             op0=ALU.mult, op1=ALU.max), reads=[bdn], writes=[bdn])
                P.op("dve", lambda e: e.tensor_scalar(out=dn[:, 2 + d], in0=dn[:, 2 + d], scalar1=1.0, scalar2=None, op0=ALU.max),
                     reads=[bdn], writes=[bdn])
                P.op("dve", lambda e: e.reciprocal(out=dn[:, 2 + d], in_=dn[:, 2 + d]), reads=[bdn], writes=[bdn])
                tgt, btg = (HX, bHX) if d == 0 else (HT, bHT)
                for sl in range(H):
                    P.op("act", lambda e, sl=sl: e.activation(
                        out=tgt[:, sl, :], in_=Hv[:, sl // 2, sl % 2, 0:128], func=AF.Copy, scale=dn[:, 2 + d, sl:sl + 1]),
                        reads=[PB[4 + sl // 2], bdn], writes=[btg])

            def b_tail(j):
                qT, kT, va, so, bR = rec_views(j)
                P.op("dve", lambda e: e.tensor_tensor(out=HX[:], in0=HX[:], in1=HT[:], op=ALU.add), reads=[bHX, bHT], writes=[bHX])
                P.op("act", lambda e: e.activation(out=HT[:], in_=HX[:], func=AF.Square), reads=[bHX], writes=[bHT])
                P.op("dve", lambda e: e.tensor_reduce(out=dn[:, 0], in_=HT[:], axis=AX.X, op=ALU.add), reads=[bHT], writes=[bdn])
                P.op("dve", lambda e: e.tensor_scalar(out=dn[:, 1], in0=dn[:, 0], scalar1=1.0 / DV, scalar2=EPS, op0=ALU.mult, op1=ALU.add),
                     reads=[bdn], writes=[bdn])
                P.op("act", lambda e: e.activation(out=dn[:, 1], in_=dn[:, 1], func=AF.Ln), reads=[bdn], writes=[bdn])
                P.op("act", lambda e: e.activation(out=dn[:, 2], in_=dn[:, 1], func=AF.Exp, scale=-0.5), reads=[bdn], writes=[bdn])
                P.op("dve", lambda e: e.tensor_tensor(out=HT[:], in0=HX[:], in1=dn[:, 2].unsqueeze(2).to_broadcast([128, H, 128]), op=ALU.mult),
                     reads=[bHX, bdn], writes=[bHT])
                P.op("dve", lambda e: e.tensor_tensor(
                    out=yb[:].rearrange("p (a l c) -> p l a c", l=2, c=128), in0=HT[:].rearrange("p (l a) c -> p l a c", l=2),
                    in1=so.rearrange("p (a l c) -> p l a c", l=2, c=128), op=ALU.mult),
                    reads=[bHT, bR], writes=[byb])
                transpose_tile(yb, byb, lambda: yT[:], byT, 4)
                for nh in range(2):
                    for kc in range(8):
                        P.mm(lambda e, nh=nh, kc=kc: e.matmul(PS[:, 5 + nh, :], lhsT=yT[:, kc, :], rhs=WO[:, kc, nh * 512:(nh + 1) * 512],
                                                              start=(kc == 0), stop=(kc == 7)),
                             reads=[byT, bWO], writes=[PB[5 + nh]], last=(kc == 7))
                P.op("dve", lambda e: e.tensor_tensor(out=zt[:].rearrange("p (b c) -> p b c", b=2), in0=PS[:, 5:7, :],
                                                      in1=MOD[:, 2, :].rearrange("p (b c) -> p b c", b=2), op=ALU.mult),
                     reads=[PB[5], PB[6], bMOD[2]], writes=[bzt])
                xs_ = xsb[j % 2]
                P.dma("sp", xs_[:], x_d[j * 128:(j + 1) * 128, :], writes=[bxsb[j % 2]])
                P.op("dve", lambda e: e.tensor_tensor(out=xs_[:], in0=xs_[:], in1=zt[:], op=ALU.add),
                     reads=[bzt, bxsb[j % 2]], writes=[bxsb[j % 2]])
                P.dma("sp", xs_d[j], xs_[:], reads=[bxsb[j % 2]], writes=[bXS[j]])

            load_rec(0)
            load_rec(1)
            for d in range(2):
                a_z(0, d); a_brow(0, d); a_elem(0, d)
            a_s(0)
            for j in range(NT):
                nx = j + 1 < NT
                if nx:
                    a_z(j + 1, 0)
                b_h(j, 0)
                if nx:
                    a_brow(j + 1, 0)
                b_evac(j, 0)
                if nx:
                    a_elem(j + 1, 0)
                    a_z(j + 1, 1)
                b_h(j, 1)
                if nx:
                    a_brow(j + 1, 1)
                b_evac(j, 1)
                if nx:
                    a_elem(j + 1, 1)
                    a_s(j + 1)
                b_tail(j)
                if j + 2 < NT:
                    load_rec(j + 2)
            P.barrier_all()

        X = sb("X", [128, NT, D])
        for j in range(NT):
            P.dma("sp", X[:, j, :], xs_d[j], reads=[bXS[j]], writes=[bX[j]])

        def mlp(layer):
            with contextlib.ExitStack() as stk:
                def sbm(name, shape, dt=F32):
                    return stk.enter_context(nc.sbuf_tensor(un(name), list(shape), dt))
                with contextlib.ExitStack() as stk2:
                    compute_mods(layer, [3, 4, 5], MOD, bMOD, 0, stk2)
                    P.barrier_all()
                make_GS(n2g_d[layer:layer + 1, :])
                UT = sbm("UT", [128, 8, TOK], BF16); bUT = [Buf("UT%d" % i) for i in range(4)]
                bzth = [Buf("zth0"), Buf("zth1")]
                bXh = [[Buf("Xh%d_%d" % (i, k)) for k in range(2)] for i in range(NT)]
                P.barrier_all()
                W1s = [sbm("W1h%d" % i, [128, 8, 1024], BF16) for i in range(2)]; bW1s = [Buf("W1h0"), Buf("W1h1")]
                W2h = sbm("W2h", [128, 8, D], BF16); bW2 = Buf("W2h")
                H1T = sbm("H1T", [128, 8, 512], BF16); bH1 = Buf("H1T")
                rl = [sbm("rl%d" % i, [128, 512]) for i in range(2)]; brl = [Buf("rl0"), Buf("rl1")]
                w1v = w1_d[layer].rearrange("(kc p) n -> p kc n", p=128)
                w2v = w2_d[layer].rearrange("(hc p) n -> p hc n", p=128)

                def load_w1(hh):
                    for q in range(2):
                        c0 = hh * 1024 + q * 512
                        P.dma("pool", W1s[hh % 2][:, :, q * 512:(q + 1) * 512], w1v[:, :, c0:c0 + 512], writes=[bW1s[hh % 2]])

                def load_w2(hh):
                    P.dma("pool", W2h[:], w2v[:, hh * 8:(hh + 1) * 8, :], writes=[bW2])

                load_w1(0); load_w2(0); load_w1(1)
                pend = {}

                def nrm(j):
                    pend[j] = norm_tile(X[:, j, :], [bX[j]], MOD, bMOD)

                def trp(j):
                    xn_, bxn_ = pend.pop(j)
                    transpose_tile(xn_, bxn_, lambda j=j: UT[:, :, j * 128:(j + 1) * 128], bUT[j // 4], 7)

                nrm(0); nrm(1); trp(0); nrm(2); trp(1); nrm(3); trp(2); trp(3)
                n = 0
                for hh in range(4):
                    W1h, bW1 = W1s[hh % 2], bW1s[hh % 2]
                    if hh >= 1:
                        load_w2(hh)
                        if hh + 1 < 4:
                            load_w1(hh + 1)
                    for tg in range(4):
                        for hc in range(8):
                            if hh == 0 and tg + 1 < 4:
                                T = 4 * (tg + 1)
                                if hc >= 2 and hc - 2 < 4:
                                    trp(T + hc - 2)
                                if hc < 4:
                                    nrm(T + hc)
                            bank = 4 + (hc % 3 if hh == 0 else hc % 4)
                            for kc in range(8):
                                P.mm(lambda e, hc=hc, kc=kc, tg=tg, bank=bank, W1h=W1h: e.matmul(
                                    PS[:, bank, :], lhsT=W1h[:, kc, hc * 128:(hc + 1) * 128], rhs=UT[:, kc, tg * 512:(tg + 1) * 512],
                                    start=(kc == 0), stop=(kc == 7)), reads=[bW1, bUT[tg]], writes=[PB[bank]], last=(kc == 7))
                            s_ = n % 2; n += 1
                            P.op("act", lambda e, bank=bank, s_=s_: e.activation(out=rl[s_][:], in_=PS[:, bank, :], func=AF.Relu),
                                 reads=[PB[bank]], writes=[brl[s_]])
                            P.op("dve", lambda e, hc=hc, s_=s_: e.tensor_tensor(
                                out=H1T[:, hc, :], in0=rl[s_][:], in1=rl[s_][:], op=ALU.mult), reads=[brl[s_]], writes=[bH1])
                        for tl in range(4):
                            j = tg * 4 + tl
                            for nh in range(2):
                                bank = (tl * 2 + nh) % 4
                                for hc in range(8):
                                    P.mm(lambda e, hc=hc, tl=tl, nh=nh, bank=bank: e.matmul(
                                        PS[:, bank, :], lhsT=H1T[:, hc, tl * 128:(tl + 1) * 128], rhs=W2h[:, hc, nh * 512:(nh + 1) * 512],
                                        start=(hc == 0), stop=(hc == 7)), reads=[bH1, bW2], writes=[PB[bank]], last=(hc == 7))
                                P.op("dve", lambda e, nh=nh, bank=bank: e.tensor_tensor(
                                    out=zt[:, nh * 512:(nh + 1) * 512], in0=PS[:, bank, :], in1=MOD[:, 2, nh * 512:(nh + 1) * 512], op=ALU.mult),
                                    reads=[PB[bank], bMOD[2]], writes=[bzth[nh]])
                                P.op("dve", lambda e, j=j, nh=nh: e.tensor_tensor(
                                    out=X[:, j, nh * 512:(nh + 1) * 512], in0=X[:, j, nh * 512:(nh + 1) * 512],
                                    in1=zt[:, nh * 512:(nh + 1) * 512], op=ALU.add), reads=[bzth[nh], bXh[j][nh]], writes=[bXh[j][nh]])
                P.barrier_all()

        def finish(final_norm):
            evs = []
            if final_norm:
                load_vrep(fing_d[0:1, :])
            ob = [sb("ob%d" % i, [128, D]) for i in range(2)]; bob = [Buf("ob0"), Buf("ob1")]
            for j in range(NT):
                s_ = j % 2
                if final_norm:
                    P.op("act", lambda e, j=j: e.activation(out=nzs[0][:], in_=X[:, j, :], func=AF.Square, accum_out=stat[:, 0:1]),
                         reads=[bX[j]], writes=[bnzs[0], bstat])
                    P.op("act", lambda e: e.activation(out=stat[:, 1:2], in_=stat[:, 0:1], func=AF.Sqrt, scale=1.0 / D, bias=EPS),
                         reads=[bstat], writes=[bstat])
                    P.op("dve", lambda e: e.reciprocal(out=stat[:, 2:3], in_=stat[:, 1:2]), reads=[bstat], writes=[bstat])
                    P.op("dve", lambda e, j=j, s_=s_: e.scalar_tensor_tensor(out=ob[s_][:], in0=X[:, j, :], scalar=stat[:, 2:3], in1=vrep[:],
                                                                             op0=ALU.mult, op1=ALU.mult),
                         reads=[bX[j], bstat, bvrep], writes=[bob[s_]])
                    evs.append(P.dma("sp", out_d[j * 128:(j + 1) * 128, :], ob[s_][:], reads=[bob[s_]]))
                else:
                    evs.append(P.dma("sp", out_d[j * 128:(j + 1) * 128, :], X[:, j, :], reads=[bX[j]]))
            for ev in evs:
                P.wait_event("sp", ev)

        if stop in ("mix0", "comb", "b2a"):
            finish(False); return nc
        mlp(0)
        if stop == "l0":
            finish(False); return nc

        with contextlib.ExitStack() as l1:
            def sb1(name, shape, dt=F32):
                return l1.enter_context(nc.sbuf_tensor(un(name), list(shape), dt))
            with contextlib.ExitStack() as stk:
                compute_mods(1, [0, 1, 2], MOD, bMOD, 0, stk)
                P.barrier_all()
            make_GS(n1g_d[1:2, :])
            PMT = sb1("PMT", [128, 4, 128], BF16); PW = sb1("PW", [128, 4, 2, 256], BF16); bPC = Buf("poolc")
            P.dma("pool", PMT[:], pmt_d[:, :, :], writes=[bPC])
            P.dma("pool", PW[:], pw_d.rearrange("g (cc p) d -> p g cc d", p=128), writes=[bPC])
            GP = sb1("GP", [128, D]); bGP = Buf("GP")
            load_vrep(psc_d[0:1, :])
            P.op("dve", lambda e: e.tensor_tensor(out=GP[:], in0=vrep[:], in1=MOD[:, 2, :], op=ALU.mult), reads=[bvrep, bMOD[2]], writes=[bGP])
            PTbs = [sb1("PTb%d" % i, [128, 8, 128], BF16) for i in range(2)]; bPTbs = [Buf("PTb0"), Buf("PTb1")]
            nxt_ = norm_tile(X[:, 0, :], [bX[0]], MOD, bMOD)
            for j in range(NT):
                xnb, bxnb = nxt_
                if j + 1 < NT:
                    nxt_ = norm_tile(X[:, j + 1, :], [bX[j + 1]], MOD, bMOD)
                PTb, bPTb = PTbs[j % 2], bPTbs[j % 2]
                for gc in range(8):
                    g = gc // 2
                    bank = gc // 4
                    P.mm(lambda e, gc=gc, g=g, bank=bank: e.matmul(
                        PS[:, bank, (gc % 4) * 128:(gc % 4 + 1) * 128], lhsT=xnb[:, gc * 128:(gc + 1) * 128], rhs=PMT[:, g, :],
                        start=True, stop=True), reads=[bxnb, bPC], writes=[PB[bank]], last=(gc % 4 == 3))
                P.op("act", lambda e, PTb=PTb: e.activation(out=PTb[:], in_=PS[:, 0:2, :].rearrange("p b (a t) -> p (b a) t", t=128), func=AF.Copy),
                     reads=[PB[0], PB[1]], writes=[bPTb])
                for g in range(4):
                    bank = 2 + g // 2
                    for cc in range(2):
                        P.mm(lambda e, g=g, cc=cc, bank=bank, PTb=PTb: e.matmul(
                            PS[:, bank, (g % 2) * 256:(g % 2 + 1) * 256], lhsT=PTb[:, g * 2 + cc, :], rhs=PW[:, g, cc, :],
                            start=(cc == 0), stop=(cc == 1)), reads=[bPTb, bPC], writes=[PB[bank]], last=(cc == 1))
                P.op("dve", lambda e: e.tensor_tensor(out=zt[:].rearrange("p (b c) -> p b c", b=2), in0=PS[:, 2:4, :],
                                                      in1=GP[:].rearrange("p (b c) -> p b c", b=2), op=ALU.mult),
                     reads=[PB[2], PB[3], bGP], writes=[bzt])
                P.op("dve", lambda e, j=j: e.tensor_tensor(out=X[:, j, :], in0=X[:, j, :], in1=zt[:], op=ALU.add),
                     reads=[bzt, bX[j]], writes=[bX[j]])
            P.barrier_all()
        if stop == "mix1":
            finish(False); return nc
        mlp(1)
        finish(True)
    return nc


def _core_inputs(i, inp, consts):
    b, s = i // 4, i % 4
    f = lambda a: np.ascontiguousarray(np.asarray(a, dtype=np.float32))
    m = {}
    m["x"] = f(inp["x"][b, s * TOK:(s + 1) * TOK])
    m["ctx"] = f(inp["ctx"][b])
    cT = np.stack([np.asarray(inp["c"][b]).reshape(8, 128).T, np.asarray(inp["c_ctx"]).reshape(8, 128).T], -1)
    m["cT"] = f(cT)
    m["ada_w"] = f(inp["ada_w"]); m["ada_b"] = f(inp["ada_b"])
    m["norm1_g"] = f(inp["norm1_g"]); m["norm2_g"] = f(inp["norm2_g"])
    m["final_g"] = f(np.asarray(inp["final_g"]).reshape(1, D))
    m["mlstm_norm_g"] = f(np.asarray(inp["mlstm_norm_g"]).reshape(1, D))
    m["pool_scale"] = f(np.asarray(inp["pool_scale"]).reshape(1, D))
    m["gate_b"] = f(np.asarray(inp["mlstm_gate_b"]).reshape(1, 32))
    m["w_in"] = f(inp["mlstm_w_in"][0]); m["w_out"] = f(inp["mlstm_w_out"][0])
    m["pool_w"] = f(inp["pool_w"][0])
    m["w1"] = f(inp["mlp_w1"]); m["w2"] = f(inp["mlp_w2"])
    m.update(consts)
    mk = np.zeros((128, 3, 4), np.float32)
    xo = []
    for k in range(3):
        o = (s + k + 1) % 4
        mf = 1.0 if o < s else 0.0
        mb = 1.0 if o > s else 0.0
        mk[:, k] = [mf, 1.0 - mf, mb, 1.0 - mb]
        xo.append(np.asarray(inp["x"][b, o * TOK:(o + 1) * TOK], dtype=np.float32))
    m["segmask"] = mk
    m["x_oth"] = np.ascontiguousarray(np.stack(xo, 0))
    return m


_CACHE = {}


def _prog(mode, stop=None):
    key = (mode, stop)
    if key not in _CACHE:
        _CACHE[key] = build(mode, stop)
    return _CACHE[key]


def kernel(x, c, ctx, c_ctx, ada_w, ada_b, norm1_g, norm2_g, mlstm_w_in, mlstm_gate_b, mlstm_norm_g,
           mlstm_w_out, pool_w, pool_scale, mlp_w1, mlp_w2, final_g, _mode="fused", _stop=None):
    inp = dict(x=x, c=c, ctx=ctx, c_ctx=c_ctx, ada_w=ada_w, ada_b=ada_b, norm1_g=norm1_g, norm2_g=norm2_g,
               mlstm_w_in=mlstm_w_in, mlstm_gate_b=mlstm_gate_b, mlstm_norm_g=mlstm_norm_g, mlstm_w_out=mlstm_w_out,
               pool_w=pool_w, pool_scale=pool_scale, mlp_w1=mlp_w1, mlp_w2=mlp_w2, final_g=final_g)
    consts = _consts()
    maps = [_core_inputs(i, inp, consts) for i in range(8)]
    res = run_bass_kernel_spmd(_prog("fused", _stop), maps, core_ids=list(range(8)))
    out = np.zeros((2, 8192, D), np.float32)
    for i in range(8):
        out[i // 4, (i % 4) * TOK:(i % 4 + 1) * TOK] = np.asarray(res.results[i]["out"])
    return out
```

```python
import contextlib
import numpy as np
import ml_dtypes
import concourse.bass as bass
import concourse.mybir as mybir
from concourse.bass_utils import run_bass_kernel_spmd

F32 = mybir.dt.float32
BF16 = mybir.dt.bfloat16
AF = mybir.ActivationFunctionType
ALU = mybir.AluOpType
AX = mybir.AxisListType

D = 1024
NT = 16
TOK = 2048
H = 8
DQK = 64
DV = 128
DVA = 129
INW = 3104
DFF = 4096
EPS = 1e-6
CAP = 15.0
NEG = -30000.0
SUMW = 1032 + 8


class Buf:
    __slots__ = ("name", "w", "r", "excl")

    def __init__(self, name, excl=False):
        self.name = name
        self.excl = excl
        self.w = None
        self.r = []


class Eng:
    def __init__(self, name, h, sem):
        self.name, self.h, self.sem = name, h, sem
        self.count = 0
        self.known = {}


class Prog:
    def __init__(self, nc, es):
        self.nc = nc
        self.es = es
        self.sems = {}
        self.E = {}
        for name, h in (("pe", nc.tensor), ("act", nc.scalar), ("dve", nc.vector),
                        ("pool", nc.gpsimd), ("sp", nc.sync)):
            sem = es.enter_context(nc.semaphore("s_" + name))
            self.sems["s_" + name] = sem
            self.E[name] = Eng(name, h, sem)
        self.dq = {}
        for q, nq in (("sp", 24), ("pool", 12)):
            lst = []
            for i in range(nq):
                nm = "d_%s%d" % (q, i)
                sem = es.enter_context(nc.semaphore(nm))
                self.sems[nm] = sem
                lst.append([nm, 0])
            self.dq[q] = [lst, 0]

    def _need(self, eng, ev, hazard):
        if ev is None:
            return
        key, val, owner = ev
        if owner == eng.name and hazard != "raw":
            return
        if owner is not None and owner != eng.name:
            assert val <= self.E[owner].count, "wait on un-incremented event %s %d" % (owner, val)
        if eng.known.get(key, 0) >= val:
            return
        eng.h.wait_ge(self.sems[key], val)
        eng.known[key] = val

    def _deps(self, eng, reads, writes):
        for b in reads:
            self._need(eng, b.w, "raw")
            if b.excl:
                for ev in b.r:
                    if ev[2] != eng.name:
                        self._need(eng, ev, "raw")
        for b in writes:
            self._need(eng, b.w, "waw")
            for ev in b.r:
                self._need(eng, ev, "war")

    def _commit(self, ev, reads, writes):
        for b in writes:
            b.w = ev
            b.r = []
        for b in reads:
            if b in writes:
                continue
            b.r = [e for e in b.r if e[0] != ev[0]] + [ev]

    def op(self, engname, fn, reads=(), writes=()):
        eng = self.E[engname]
        self._deps(eng, reads, writes)
        ins = fn(eng.h)
        eng.count += 1
        ins.then_inc(eng.sem, 1)
        ev = ("s_" + engname, eng.count, engname)
        self._commit(ev, reads, writes)

    def mm(self, fn, reads=(), writes=(), last=True):
        eng = self.E["pe"]
        self._deps(eng, reads, writes)
        ins = fn(eng.h)
        ev = ("s_pe", eng.count + 1, "pe")
        if last:
            eng.count += 1
            ins.then_inc(eng.sem, 1)
        self._commit(ev, reads, writes)

    def dma(self, q, out, in_, reads=(), writes=()):
        eng = self.E[q]
        self._deps(eng, reads, writes)
        lst, idx = self.dq[q]
        ent = lst[idx % len(lst)]
        self.dq[q][1] = idx + 1
        if ent[1] > 0 and eng.known.get(ent[0], 0) < ent[1]:
            eng.h.wait_ge(self.sems[ent[0]], ent[1])
            eng.known[ent[0]] = ent[1]
        ent[1] += 16
        eng.h.dma_start(out=out, in_=in_).then_inc(self.sems[ent[0]], 16)
        ev = (ent[0], ent[1], None)
        self._commit(ev, reads, writes)
        return ev

    def wait_event(self, engname, ev):
        self._need(self.E[engname], ev, "raw")

    def barrier_all(self, engs=("pe", "act", "dve", "pool", "sp")):
        for a in engs:
            ea = self.E[a]
            for b in engs:
                if a == b:
                    continue
                eb = self.E[b]
                if eb.count > 0 and ea.known.get("s_" + b, 0) < eb.count:
                    ea.h.wait_ge(eb.sem, eb.count)
                    ea.known["s_" + b] = eb.count


def _pool_mats():
    out = np.zeros((4, 128, 128), np.float32)
    for gi, win in enumerate((2, 4, 8, 16)):
        pm = np.zeros((128, 128), np.float32)
        for r in range(2):
            for t in range(64):
                lo = min(max(t - win // 2, 0), 64)
                hi = min(max(t - win // 2 + win, 0), 64)
                pm[r * 64 + t, r * 64 + lo:r * 64 + hi] = 1.0 / (hi - lo)
        out[gi] = (pm - np.eye(128, dtype=np.float32)).T
    return out


def _consts():
    c = {}
    c["ident"] = np.eye(128, dtype=np.float32)
    s = np.arange(128)[:, None]
    t = np.arange(128)[None, :]
    c["utri"] = (s <= t).astype(np.float32)
    c["ltri"] = (s >= t).astype(np.float32)
    c["ones"] = np.ones((128, 128), np.float32)
    negf = np.where(s <= t, 0.0, NEG).astype(np.float32)
    negb = np.where(s >= t, 0.0, NEG).astype(np.float32)
    c["negm"] = np.ascontiguousarray(np.broadcast_to(np.stack([negf, negb], 1)[:, :, None, :], (128, 2, 8, 128)))
    c["pmt"] = np.ascontiguousarray(_pool_mats().transpose(1, 0, 2))
    return c


def build(mode="fused", stop=None):
    nc = bass.Bass("TRN2", target_bir_lowering=False)

    def din(name, shape, dt=F32):
        return nc.dram_tensor(name, list(shape), dt, kind="ExternalInput").ap()

    x_d = din("x", [TOK, D])
    ctx_d = din("ctx", [256, D])
    cT_d = din("cT", [128, 8, 2])
    adaw_d = din("ada_w", [2, D, 6 * D])
    adab_d = din("ada_b", [2, 6 * D])
    n1g_d = din("norm1_g", [2, D])
    n2g_d = din("norm2_g", [2, D])
    fing_d = din("final_g", [1, D])
    mng_d = din("mlstm_norm_g", [1, D])
    psc_d = din("pool_scale", [1, D])
    gb_d = din("gate_b", [1, 32])
    win_d = din("w_in", [D, INW])
    wout_d = din("w_out", [D, D])
    pw_d = din("pool_w", [4, 256, 256])
    w1_d = din("w1", [2, D, DFF])
    w2_d = din("w2", [2, DFF, D])
    ident_d = din("ident", [128, 128])
    utri_d = din("utri", [128, 128])
    ltri_d = din("ltri", [128, 128])
    ones_d = din("ones", [128, 128])
    negm_d = din("negm", [128, 2, 8, 128])
    pmt_d = din("pmt", [128, 4, 128])
    mk_d = din("segmask", [128, 3, 4])
    xoth_d = din("x_oth", [3, TOK, D])
    out_d = nc.dram_tensor("out", [TOK, D], F32, kind="ExternalOutput").ap()
    RECW = 512 + 512 + 512 + H * DVA + 1024
    rec_d = nc.dram_tensor("rec", [NT, 128, RECW], BF16, kind="Internal").ap()

    _cnt = [0]

    def un(name):
        _cnt[0] += 1
        return "sb%d_%s" % (_cnt[0], name)

    es = contextlib.ExitStack()
    with es:
        P = Prog(nc, es)

        def sb(name, shape, dt=F32):
            return es.enter_context(nc.sbuf_tensor(un(name), list(shape), dt))

        PS = es.enter_context(nc.psum_tensor("ps", [128, 8, 512], F32))
        PB = [Buf("psb%d" % i, excl=True) for i in range(8)]

        bX = [Buf("X%d" % i) for i in range(NT)]
        xs_d = nc.dram_tensor("xs_scr", [NT, 128, D], F32, kind="Internal").ap()
        bXS = [Buf("XS%d" % i) for i in range(NT)]
        identf = sb("identf", [128, 128]); identb = sb("identb", [128, 128], BF16)
        utri = sb("utri", [128, 128]); ltri = sb("ltri", [128, 128]); ones = sb("ones", [128, 128])
        bC = Buf("consts")
        MOD = sb("MOD", [128, 3, D]); bMOD = [Buf("MOD%d" % i) for i in range(3)]
        vrep = sb("vrep", [128, D]); bvrep = Buf("vrep")
        stats = [sb("stat%d" % i, [128, 8]) for i in range(2)]; bstats = [Buf("stat0"), Buf("stat1")]
        stat, bstat = stats[0], bstats[0]
        nzs = [sb("nz%d" % i, [128, D]) for i in range(2)]; bnzs = [Buf("nz0"), Buf("nz1")]
        zt = sb("zt", [128, D]); bzt = Buf("zt")
        xnbs = [sb("xnb%d" % i, [128, D], BF16) for i in range(2)]; bxnbs = [Buf("xnb0"), Buf("xnb1")]
        nrm_ = [0]
        cTs = sb("cTs", [128, 8, 2]); bcT = Buf("cTs")
        screp = sb("screp", [128, 2, 8, 128]); bscrep = Buf("screp")

        for dst, src in ((identf, ident_d), (utri, utri_d), (ltri, ltri_d), (ones, ones_d)):
            P.dma("sp", dst[:], src[:, :], writes=[bC])
        P.dma("pool", identb[:], ident_d[:, :], writes=[bC])
        P.dma("sp", cTs[:], cT_d[:, :, :], writes=[bcT])

        P.op("act", lambda e: e.activation(out=cTs[:], in_=cTs[:], func=AF.Silu), reads=[bcT], writes=[bcT])
        for j in range(2):
            for kc in range(8):
                P.op("dve", lambda e, j=j, kc=kc: e.tensor_scalar(
                    out=screp[:, j, kc, :], in0=ones[:], scalar1=cTs[:, kc, j:j + 1], scalar2=None,
                    op0=ALU.mult), reads=[bcT, bC], writes=[bscrep])

        def compute_mods(layer, blocks, dst, dbufs, cidx, stk, second=None):
            astg = [stk.enter_context(nc.sbuf_tensor(un("astg%d" % i), [128, 8, 512], F32)) for i in range(2)]
            bast = [Buf("astg%d" % i) for i in range(2)]
            abr = stk.enter_context(nc.sbuf_tensor(un("abr"), [128, 2, 512], F32)); babr = [Buf("abr0"), Buf("abr1")]
            n = 0
            for i, blk in enumerate(blocks):
                for hf in range(2):
                    c0 = blk * D + hf * 512
                    sl = n % 2
                    P.dma("sp", astg[sl][:], adaw_d[layer].rearrange("(kc p) n -> p kc n", p=128)[:, :, c0:c0 + 512],
                          writes=[bast[sl]])
                    P.dma("sp", abr[:, sl, :], pbc(adab_d[layer:layer + 1, c0:c0 + 512]),
                          writes=[babr[sl]])
                    bank = 6 + sl
                    for kc in range(8):
                        P.mm(lambda e, kc=kc, sl=sl, bank=bank: e.matmul(
                            PS[:, bank, 0:512], lhsT=screp[:, cidx, kc, :], rhs=astg[sl][:, kc, :],
                            start=(kc == 0), stop=(kc == 7)),
                            reads=[bscrep, bast[sl]], writes=[PB[bank]], last=(kc == 7))
                    P.op("dve", lambda e, i=i, hf=hf, sl=sl, bank=bank: e.tensor_tensor(
                        out=dst[:, i, hf * 512:(hf + 1) * 512], in0=PS[:, bank, 0:512], in1=abr[:, sl, :], op=ALU.add),
                        reads=[PB[bank], babr[sl]], writes=[dbufs[i]])
                    if second is not None and i < second[3]:
                        dst2, dbufs2, cidx2 = second[0], second[1], second[2]
                        bank2 = 4 + sl
                        for kc in range(8):
                            P.mm(lambda e, kc=kc, sl=sl, bank2=bank2: e.matmul(
                                PS[:, bank2, 0:512], lhsT=screp[:, cidx2, kc, :], rhs=astg[sl][:, kc, :],
                                start=(kc == 0), stop=(kc == 7)),
                                reads=[bscrep, bast[sl]], writes=[PB[bank2]], last=(kc == 7))
                        P.op("dve", lambda e, i=i, hf=hf, sl=sl, bank2=bank2: e.tensor_tensor(
                            out=dst2[:, i, hf * 512:(hf + 1) * 512], in0=PS[:, bank2, 0:512], in1=abr[:, sl, :], op=ALU.add),
                            reads=[PB[bank2], babr[sl]], writes=[dbufs2[i]])
                    n += 1

        def pbc(row_ap):
            a = row_ap.partition_broadcast(128)
            if len(a.shape) == 3:
                a = a.rearrange("p o n -> p (o n)")
            return a

        def load_vrep(src_row_ap):
            P.dma("sp", vrep[:], pbc(src_row_ap), writes=[bvrep])

        def make_GS(gain_row_ap, M=None, bM=None):
            M = MOD if M is None else M
            bM = bMOD if bM is None else bM
            load_vrep(gain_row_ap)
            P.op("dve", lambda e: e.scalar_tensor_tensor(
                out=M[:, 1, :], in0=M[:, 1, :], scalar=1.0, in1=vrep[:], op0=ALU.add, op1=ALU.mult),
                reads=[bM[1], bvrep], writes=[bM[1]])

        def norm_tile(src_ap, src_bufs, gs, gsbuf):
            i = nrm_[0] % 2; nrm_[0] += 1
            st_, bst_, nz_, bnz_, xo_, bxo_ = stats[i], bstats[i], nzs[i], bnzs[i], xnbs[i], bxnbs[i]
            P.op("act", lambda e: e.activation(out=nz_[:], in_=src_ap, func=AF.Square, accum_out=st_[:, 0:1]),
                 reads=list(src_bufs), writes=[bnz_, bst_])
            P.op("act", lambda e: e.activation(out=st_[:, 1:2], in_=st_[:, 0:1], func=AF.Sqrt, scale=1.0 / D, bias=EPS),
                 reads=[bst_], writes=[bst_])
            P.op("dve", lambda e: e.reciprocal(out=st_[:, 2:3], in_=st_[:, 1:2]), reads=[bst_], writes=[bst_])
            P.op("dve", lambda e: e.scalar_tensor_tensor(out=nz_[:], in0=src_ap, scalar=st_[:, 2:3], in1=gs[:, 1, :],
                                                         op0=ALU.mult, op1=ALU.mult),
                 reads=list(src_bufs) + [bst_] + list(gsbuf), writes=[bnz_])
            P.op("dve", lambda e: e.tensor_tensor(out=xo_[:], in0=nz_[:], in1=gs[:, 0, :], op=ALU.add),
                 reads=[bnz_] + list(gsbuf), writes=[bxo_])
            return xo_, bxo_

        PSB = [PS[:, i, :].bitcast(BF16) for i in range(8)]

        def transpose_tile(src_bf, src_buf, dstT_ap_fn, dst_buf, bank):
            for kc in range(8):
                P.mm(lambda e, kc=kc: e.transpose(out=PSB[bank][:, kc * 128:(kc + 1) * 128],
                                                  in_=src_bf[:, kc * 128:(kc + 1) * 128], identity=identb[:]),
                     reads=[src_buf, bC], writes=[PB[bank]], last=(kc == 7))
            P.op("act", lambda e: e.activation(out=dstT_ap_fn(), in_=PSB[bank].rearrange("p (k t) -> p k t", k=8),
                                               func=AF.Copy), reads=[PB[bank]], writes=[dst_buf])

        l0 = contextlib.ExitStack()
        with l0:
            def sb0(name, shape, dt=F32):
                return l0.enter_context(nc.sbuf_tensor(un(name), list(shape), dt))
            MODC = sb0("MODC", [128, 2, D]); bMODC = [Buf("MODC0"), Buf("MODC1")]

            NTA = NT + 2
            gbrep = sb0("gbrep", [128, 32]); bgb = Buf("gbrep")
            P.dma("sp", gbrep[:], pbc(gb_d[0:1, :]), writes=[bgb])
            GRAW = sb0("GRAW", [128, NTA, 32]); bGRAW = Buf("GRAW")
            CK = sb0("CK", [128, 2, 512], BF16); CV = sb0("CVa", [128, 2, H, DVA], BF16); bCKV = Buf("CKV")
            P.op("pool", lambda e: e.memset(CV[:, :, :, 128:129], 1.0), writes=[bCKV])
            bRECD = [Buf("recd%d" % i) for i in range(NT)]
            bRECQ = [Buf("recq%d" % i) for i in range(NT)]
            OQ, OK_, OKT, OV, OSO = 0, 512, 1024, 1536, 1536 + H * DVA
            TMW = RECW - OKT

            SALL = sb0("SALL", [128, 3, SUMW]); bSALL = Buf("SALL")
            GC = sb0("GC", [128, NTA, 4, 8]); E1 = sb0("E1", [128, NTA, 4, 8]); bG = Buf("gates")
            LF = sb0("LF", [128, NTA, 2, 8]); Bc = sb0("Bc", [128, NTA, 2, 8]); BT = sb0("BT", [128, NTA, 2, 8])
            CVc = sb0("CVc", [128, NTA, 2, 8]); WALL = sb0("WALL", [128, NTA, 2, 8]); FT = sb0("FT", [128, NTA, 2, 8])
            FS = sb0("FS", [128, NTA, 2, 4]); BTX = sb0("BTX", [128, NT, 2, 8]); PFS = sb0("PFS", [128, NT, 2, 4])
            FSEG = sb0("FSEG", [128, 2, 4]); BSUM = sb0("BSUM", [128, 2, 8])
            WSG = sb0("WSG", [128, NT, 2, 8])
            ST = sb0("ST", [128, 2, 4, DVA]); bST = Buf("ST")
            STC = sb0("STC", [128, 2, 4, DVA]); bSTC = Buf("STC")
            WVt = [sb0("WVt%d" % i, [128, H, DVA], BF16) for i in range(2)]; bWV = [Buf("WV0"), Buf("WV1")]
            KVb = [sb0("KVb%d" % i, [128, 512 + H * DVA], BF16) for i in range(4)]; bKV = [Buf("KVb%d" % i) for i in range(4)]
            bCS = [Buf("CS%d" % i) for i in range(NT)]

            def gates_pass(nta):
                graw4 = GRAW[:, 0:nta].rearrange("p t (y h) -> p t y h", y=4)
                P.op("act", lambda e: e.activation(out=GC[:, 0:nta], in_=graw4, func=AF.Tanh, scale=1.0 / CAP), reads=[bGRAW], writes=[bG])
                P.op("act", lambda e: e.activation(out=E1[:, 0:nta], in_=GC[:, 0:nta], func=AF.Exp, scale=-CAP), reads=[bG], writes=[bG])
                P.op("dve", lambda e: e.tensor_scalar(out=E1[:, 0:nta], in0=E1[:, 0:nta], scalar1=1.0, scalar2=None, op0=ALU.add), reads=[bG], writes=[bG])
                P.op("act", lambda e: e.activation(out=E1[:, 0:nta], in_=E1[:, 0:nta], func=AF.Ln), reads=[bG], writes=[bG])
                for d in range(2):
                    P.op("dve", lambda e, d=d: e.tensor_scalar(out=LF[:, 0:nta, d, :], in0=E1[:, 0:nta, 2 * d + 1, :], scalar1=-1.0,
                                                               scalar2=None, op0=ALU.mult), reads=[bG], writes=[bG])
                GPS = PS[:, 0, :].rearrange("p (t x d h) -> p t x d h", t=16, x=2, d=2)
                for t0, ntl in (((0, 16), (16, 2)) if nta > 16 else ((0, 16),)):
                    for tt in range(ntl):
                        j = t0 + tt
                        P.mm(lambda e, j=j, tt=tt: e.matmul(GPS[:, tt, 0, 0, :], lhsT=utri[:], rhs=LF[:, j, 0, :], start=True, stop=True),
                             reads=[bG, bC], writes=[PB[0]], last=False)
                        P.mm(lambda e, j=j, tt=tt: e.matmul(GPS[:, tt, 0, 1, :], lhsT=ltri[:], rhs=LF[:, j, 1, :], start=True, stop=True),
                             reads=[bG, bC], writes=[PB[0]], last=False)
                        P.mm(lambda e, j=j, tt=tt: e.matmul(GPS[:, tt, 1, :, :], lhsT=ones[:], rhs=LF[:, j, :, :], start=True, stop=True),
                             reads=[bG, bC], writes=[PB[0]], last=(tt == ntl - 1))
                    P.op("dve", lambda e, t0=t0, ntl=ntl: e.tensor_copy(out=Bc[:, t0:t0 + ntl], in_=GPS[:, 0:ntl, 0]),
                         reads=[PB[0]], writes=[bG])
                    P.op("dve", lambda e, t0=t0, ntl=ntl: e.tensor_copy(out=BT[:, t0:t0 + ntl], in_=GPS[:, 0:ntl, 1]),
                         reads=[PB[0]], writes=[bG])
                for d in range(2):
                    P.op("dve", lambda e, d=d: e.scalar_tensor_tensor(
                        out=CVc[:, 0:nta, d, :], in0=GC[:, 0:nta, 2 * d, :], scalar=CAP, in1=Bc[:, 0:nta, d, :], op0=ALU.mult, op1=ALU.subtract),
                        reads=[bG], writes=[bG])
                P.op("dve", lambda e: e.tensor_tensor(out=WALL[:, 0:nta], in0=BT[:, 0:nta], in1=CVc[:, 0:nta], op=ALU.add), reads=[bG], writes=[bG])
                P.op("act", lambda e: e.activation(out=WALL[:, 0:nta], in_=WALL[:, 0:nta], func=AF.Exp), reads=[bG], writes=[bG])
                P.op("act", lambda e: e.activation(out=FT[:, 0:nta], in_=BT[:, 0:nta], func=AF.Exp), reads=[bG], writes=[bG])
                for hl in range(2):
                    FTv = FT[:, 0:nta].rearrange("p t d (hp l) -> p t d hp l", l=2)
                    P.op("dve", lambda e, hl=hl, FTv=FTv: e.tensor_copy(out=FS[hl * 64:(hl + 1) * 64, 0:nta], in_=FTv[hl * 64:(hl + 1) * 64, :, :, :, hl]),
                         reads=[bG], writes=[bG])
                P.op("dve", lambda e: e.memset(BTX[:, 0, 0, :], 0.0), writes=[bG])
                P.op("dve", lambda e: e.memset(BTX[:, NT - 1, 1, :], 0.0), writes=[bG])
                for j in range(1, NT):
                    P.op("dve", lambda e, j=j: e.tensor_tensor(out=BTX[:, j, 0, :], in0=BTX[:, j - 1, 0, :], in1=BT[:, j - 1, 0, :], op=ALU.add),
                         reads=[bG], writes=[bG])
                    jb = NT - 1 - j
                    P.op("dve", lambda e, jb=jb: e.tensor_tensor(out=BTX[:, jb, 1, :], in0=BTX[:, jb + 1, 1, :], in1=BT[:, jb + 1, 1, :], op=ALU.add),
                         reads=[bG], writes=[bG])
                P.op("dve", lambda e: e.tensor_tensor(out=BSUM[:, 0, :], in0=BTX[:, NT - 1, 0, :], in1=BT[:, NT - 1, 0, :], op=ALU.add), reads=[bG], writes=[bG])
                P.op("dve", lambda e: e.tensor_tensor(out=BSUM[:, 1, :], in0=BTX[:, 0, 1, :], in1=BT[:, 0, 1, :], op=ALU.add), reads=[bG], writes=[bG])
                P.op("dve", lambda e: e.tensor_tensor(out=WSG[:], in0=CVc[:, 0:NT], in1=BTX[:], op=ALU.subtract), reads=[bG], writes=[bG])
                P.op("dve", lambda e: e.tensor_tensor(out=WSG[:], in0=WSG[:], in1=BSUM[:].unsqueeze(1).to_broadcast([128, NT, 2, 8]), op=ALU.add),
                     reads=[bG], writes=[bG])
                P.op("act", lambda e: e.activation(out=WSG[:], in_=WSG[:], func=AF.Exp), reads=[bG], writes=[bG])
                P.op("act", lambda e: e.activation(out=BTX[:], in_=BTX[:], func=AF.Exp), reads=[bG], writes=[bG])
                P.op("act", lambda e: e.activation(out=BSUM[:], in_=BSUM[:], func=AF.Exp), reads=[bG], writes=[bG])
                for hl in range(2):
                    Xv = BTX[:].rearrange("p t d (hp l) -> p t d hp l", l=2)
                    Sv = BSUM[:].rearrange("p d (hp l) -> p d hp l", l=2)
                    P.op("dve", lambda e, hl=hl, Xv=Xv: e.tensor_copy(out=PFS[hl * 64:(hl + 1) * 64], in_=Xv[hl * 64:(hl + 1) * 64, :, :, :, hl]),
                         reads=[bG], writes=[bG])
                    P.op("dve", lambda e, hl=hl, Sv=Sv: e.tensor_copy(out=FSEG[hl * 64:(hl + 1) * 64], in_=Sv[hl * 64:(hl + 1) * 64, :, :, hl]),
                         reads=[bG], writes=[bG])

            def st_view(t, d):
                return t[:, d].rearrange("p (a b) c -> p a (b c)", a=2)

            def ss_wv(args, slot):
                kf, vf, bkf, jf, kb, vb, bkb, jb = args
                for d, (vt, bk, j) in enumerate(((vf, bkf, jf), (vb, bkb, jb))):
                    w_ = WVt[2 * slot + d]
                    P.op("dve", lambda e, d=d, vt=vt, j=j, w_=w_: e.tensor_tensor(
                        out=w_[:], in0=vt, in1=WALL[:, j, d, :].unsqueeze(2).to_broadcast([128, H, DVA]), op=ALU.mult),
                        reads=[bk, bG], writes=[bWV[2 * slot + d]])

            def ss_mm(args, slot):
                kf, vf, bkf, jf, kb, vb, bkb, jb = args
                for d, (kt, bk) in enumerate(((kf, bkf), (kb, bkb))):
                    w_ = WVt[2 * slot + d]
                    for h in range(H):
                        hp, hl = h // 2, h % 2
                        bank = 4 + 2 * d + hp // 2
                        c0 = (hp % 2) * DVA
                        P.mm(lambda e, kt=kt, h=h, hl=hl, bank=bank, c0=c0, w_=w_: e.matmul(
                            PS[hl * 64:(hl + 1) * 64, bank, c0:c0 + DVA], lhsT=kt[:, h * 64:(h + 1) * 64], rhs=w_[:, h, :],
                            start=True, stop=True), reads=[bk, bWV[2 * slot + d]], writes=[PB[bank]], last=(h == H - 1))

            def ss_st(args):
                kf, vf, bkf, jf, kb, vb, bkb, jb = args
                for d, j in ((0, jf), (1, jb)):
                    P.op("dve", lambda e, d=d, j=j: e.tensor_tensor(
                        out=ST[:, d], in0=ST[:, d], in1=FS[:, j, d, :].unsqueeze(2).to_broadcast([128, 4, DVA]), op=ALU.mult),
                        reads=[bST, bG], writes=[bST])
                    P.op("dve", lambda e, d=d: e.tensor_tensor(
                        out=st_view(ST, d), in0=st_view(ST, d), in1=PS[:, 4 + 2 * d:6 + 2 * d, 0:2 * DVA], op=ALU.add),
                        reads=[bST, PB[4 + 2 * d], PB[5 + 2 * d]], writes=[bST])

            def state_step(kf, vf, bkf, jf, kb, vb, bkb, jb):
                args = (kf, vf, bkf, jf, kb, vb, bkb, jb)
                ss_wv(args, 0); ss_mm(args, 0); ss_st(args)

            def load_kv(j, slot):
                P.dma("sp", KVb[slot][:], rec_d[j][:, OKT:OKT + 512 + H * DVA], reads=[bRECD[j]], writes=[bKV[slot]])

            def kv_views(slot):
                t = KVb[slot]
                return t[:, 0:512], t[:, 512:].rearrange("p (h c) -> p h c", c=DVA)

            def b1_pass(cs_tile):
                def step_args(stp):
                    jf, jb = stp, NT - 1 - stp
                    sf, sbk = (2 * stp) % 4, (2 * stp + 1) % 4
                    kf, vf = kv_views(sf); kb, vb = kv_views(sbk)
                    return (kf, vf, bKV[sf], jf, kb, vb, bKV[sbk], jb)
                load_kv(0, 0); load_kv(NT - 1, 1)
                load_kv(1, 2); load_kv(NT - 2, 3)
                ss_wv(step_args(0), 0)
                for stp in range(NT):
                    jf, jb = stp, NT - 1 - stp
                    args = step_args(stp)
                    if cs_tile is not None:
                        P.op("act", lambda e, jf=jf: e.activation(out=cs_tile[:, jf, 0], in_=ST[:, 0], func=AF.Copy), reads=[bST], writes=[bCS[jf]])
                        P.op("act", lambda e, jb=jb: e.activation(out=cs_tile[:, jb, 1], in_=ST[:, 1], func=AF.Copy), reads=[bST], writes=[bCS[jb]])
                    ss_mm(args, stp % 2)
                    if stp + 1 < NT:
                        ss_wv(step_args(stp + 1), (stp + 1) % 2)
                    ss_st(args)
                    if stp + 2 < NT:
                        load_kv(stp + 2, (2 * stp) % 4); load_kv(NT - 3 - stp, (2 * stp + 1) % 4)

            MK = sb0("MK", [128, 3, 4]); bMK = Buf("MK")
            P.dma("sp", MK[:], mk_d[:, :, :], writes=[bMK])
            WE = sb0("WE", [128, NT, 8]); bWE = Buf("WE")

            def b1_lite(k):
                P.op("dve", lambda e: e.tensor_scalar(out=WE[:], in0=WSG[:, :, 0, :], scalar1=MK[:, k, 0:1], scalar2=None, op0=ALU.mult),
                     reads=[bG, bMK], writes=[bWE])
                P.op("dve", lambda e: e.scalar_tensor_tensor(out=WE[:], in0=WSG[:, :, 1, :], scalar=MK[:, k, 2:3], in1=WE[:],
                                                             op0=ALU.mult, op1=ALU.add), reads=[bG, bMK, bWE], writes=[bWE])
                for j0 in range(3):
                    load_kv(j0, j0)
                for j in range(NT):
                    if j + 3 < NT:
                        load_kv(j + 3, (j + 3) % 4)
                    kt, vt = kv_views(j % 4)
                    bk = bKV[j % 4]
                    wslot = j % 2
                    P.op("dve", lambda e, vt=vt, j=j, wslot=wslot: e.tensor_tensor(
                        out=WVt[wslot][:], in0=vt, in1=WE[:, j, :].unsqueeze(2).to_broadcast([128, H, DVA]), op=ALU.mult),
                        reads=[bk, bWE], writes=[bWV[wslot]])
                    for h in range(H):
                        hp, hl = h // 2, h % 2
                        P.mm(lambda e, kt=kt, h=h, hl=hl, hp=hp, j=j, wslot=wslot: e.matmul(
                            PS[hl * 64:(hl + 1) * 64, hp, 0:DVA], lhsT=kt[:, h * 64:(h + 1) * 64], rhs=WVt[wslot][:, h, :],
                            start=(j == 0), stop=(j == NT - 1)), reads=[bk, bWV[wslot]], writes=[PB[hp]],
                            last=(h == H - 1))
                P.op("dve", lambda e: e.tensor_copy(out=ST[:, 0], in_=PS[:, 0:4, 0:DVA]), reads=PB[0:4], writes=[bST])

            p1 = contextlib.ExitStack()
            with p1:
                def sbp(name, shape, dt=F32):
                    return p1.enter_context(nc.sbuf_tensor(un(name), list(shape), dt))

                WIN = sbp("WIN", [128, 8, INW], BF16); bWIN = Buf("WIN")
                wv = win_d.rearrange("(kc p) n -> p kc n", p=128)
                for c0 in range(0, INW, 512):
                    c1 = min(INW, c0 + 512)
                    P.dma("pool", WIN[:, :, c0:c1], wv[:, :, c0:c1], writes=[bWIN])
                with contextlib.ExitStack() as stk:
                    compute_mods(0, [0, 1, 2], MOD, bMOD, 0, stk, second=(MODC, bMODC, 1, 2))
                    P.barrier_all()
                make_GS(n1g_d[0:1, :])

                def proj_tokmajor(xT, bxT, tcol, c0, n, bank):
                    for kc in range(8):
                        P.mm(lambda e, kc=kc: e.matmul(PS[:, bank, 0:n], lhsT=xT[:, kc, tcol:tcol + 128],
                                                       rhs=WIN[:, kc, c0:c0 + n], start=(kc == 0), stop=(kc == 7)),
                             reads=[bxT, bWIN], writes=[PB[bank]], last=(kc == 7))

                with contextlib.ExitStack() as cstk:
                    make_GS(n1g_d[0:1, :], MODC, bMODC)
                    ctile = cstk.enter_context(nc.sbuf_tensor(un("ctile"), [128, D], F32)); bct = Buf("ctile")
                    cxT = cstk.enter_context(nc.sbuf_tensor(un("cxT"), [128, 8, 256], BF16)); bcxT = Buf("cxT")
                    for ct in range(2):
                        P.dma("sp", ctile[:], ctx_d[ct * 128:(ct + 1) * 128, :], writes=[bct])
                        xn_, bxn_ = norm_tile(ctile[:], [bct], MODC, bMODC)
                        transpose_tile(xn_, bxn_, lambda ct=ct: cxT[:, :, ct * 128:(ct + 1) * 128], bcxT, 7)
                    for ct in range(2):
                        proj_tokmajor(cxT, bcxT, ct * 128, 512, 512, 2)
                        P.op("act", lambda e, ct=ct: e.activation(out=CK[:, ct, :], in_=PS[:, 2, :], func=AF.Copy),
                             reads=[PB[2]], writes=[bCKV])
                        proj_tokmajor(cxT, bcxT, ct * 128, 1024, 512, 3)
                        proj_tokmajor(cxT, bcxT, ct * 128, 1536, 512, 4)
                        P.op("dve", lambda e, ct=ct: e.tensor_copy(
                            out=CV[:, ct, :, 0:128], in_=PS[:, 3:5, :].rearrange("p b (h c) -> p (b h) c", c=128)),
                            reads=[PB[3], PB[4]], writes=[bCKV])
                        proj_tokmajor(cxT, bcxT, ct * 128, 3072, 32, 7)
                        P.op("dve", lambda e, ct=ct: e.tensor_tensor(out=GRAW[:, NT + ct, :], in0=PS[:, 7, 0:32], in1=gbrep[:], op=ALU.add),
                             reads=[PB[7], bgb], writes=[bGRAW])
                    P.barrier_all()

                load_vrep(mng_d[0:1, :])
                xnT = [sbp("xnT%d" % i, [128, 8, 512], BF16) for i in range(2)]
                bxnT = [Buf("xnT0"), Buf("xnT1")]
                QKs = sbp("QKs", [128, 2, 4, 512], BF16); bQK = Buf("QKs")
                RECs = [sbp("recs%d" % i, [128, TMW], BF16) for i in range(2)]
                bREC = [Buf("recs%d" % i) for i in range(2)]
                xst = [sbp("xst%d" % i, [128, D]) for i in range(2)]; bxst = [Buf("xst0"), Buf("xst1")]

                def rec_v(r):
                    return r[:, OV - OKT:OV - OKT + H * DVA].rearrange("p (h c) -> p h c", c=DVA)

                for i, r in enumerate(RECs):
                    P.op("pool", lambda e, r=r: e.memset(rec_v(r)[:, :, 128:129], 1.0), writes=[bREC[i]])
                nrec_ = [0]

                def p1_load(xsrc, j):
                    P.dma("sp", xst[j % 2][:], xsrc[j * 128:(j + 1) * 128, :], writes=[bxst[j % 2]])

                def p1_qk(g):
                    xT, bxT = xnT[g % 2], bxnT[g % 2]
                    for which, c0 in ((0, 0), (1, 512)):
                        for hp in range(4):
                            bank = hp % 2
                            for kc in range(8):
                                P.mm(lambda e, kc=kc, hp=hp, c0=c0, bank=bank: e.matmul(
                                    PS[:, bank, :], lhsT=WIN[:, kc, c0 + hp * 128:c0 + (hp + 1) * 128],
                                    rhs=xT[:, kc, :], start=(kc == 0), stop=(kc == 7)),
                                    reads=[bxT, bWIN], writes=[PB[bank]], last=(kc == 7))
                            P.op("act", lambda e, hp=hp, bank=bank, which=which: e.activation(
                                out=QKs[:, which, hp, :], in_=PS[:, bank, :], func=AF.Copy,
                                scale=(DQK ** -0.5 if which == 0 else 1.0)),
                                reads=[PB[bank]], writes=[bQK])
                    for tl in range(4):
                        j = g * 4 + tl
                        P.dma("sp", rec_d[j][:, 0:1024].rearrange("p (w a t) -> p w a t", w=2, a=4),
                              QKs[:, :, :, tl * 128:(tl + 1) * 128], reads=[bQK], writes=[bRECQ[j]])

                def p1_proj(j, lite):
                    g, tl = j // 4, j % 4
                    xT, bxT = xnT[g % 2], bxnT[g % 2]
                    ri = j % 2
                    r = RECs[ri]
                    proj_tokmajor(xT, bxT, tl * 128, 512, 512, 2)
                    P.op("act", lambda e, r=r: e.activation(out=r[:, 0:512], in_=PS[:, 2, :], func=AF.Copy),
                         reads=[PB[2]], writes=[bREC[ri]])
                    proj_tokmajor(xT, bxT, tl * 128, 1024, 512, 3)
                    proj_tokmajor(xT, bxT, tl * 128, 1536, 512, 4)
                    P.op("dve", lambda e, r=r: e.tensor_copy(
                        out=rec_v(r)[:, :, 0:128], in_=PS[:, 3:5, :].rearrange("p b (h c) -> p (b h) c", c=128)),
                        reads=[PB[3], PB[4]], writes=[bREC[ri]])
                    if not lite:
                        proj_tokmajor(xT, bxT, tl * 128, 2048, 512, 5)
                        proj_tokmajor(xT, bxT, tl * 128, 2560, 512, 6)
                        P.op("act", lambda e: e.activation(out=zt[:].rearrange("p (b c) -> p b c", b=2), in_=PS[:, 5:7, :],
                                                           func=AF.Sigmoid), reads=[PB[5], PB[6]], writes=[bzt])
                        P.op("dve", lambda e, r=r: e.tensor_tensor(out=r[:, OSO - OKT:OSO - OKT + D], in0=zt[:], in1=vrep[:], op=ALU.mult),
                             reads=[bzt, bvrep], writes=[bREC[ri]])
                    proj_tokmajor(xT, bxT, tl * 128, 3072, 32, 1)
                    P.op("dve", lambda e, j=j: e.tensor_tensor(out=GRAW[:, j, :], in0=PS[:, 1, 0:32], in1=gbrep[:], op=ALU.add),
                         reads=[PB[1], bgb], writes=[bGRAW])

                def p1_store(j, lite):
                    ri = j % 2
                    r = RECs[ri]
                    if lite:
                        P.dma("sp", rec_d[j][:, OKT:OSO], r[:, 0:OSO - OKT], reads=[bREC[ri]], writes=[bRECD[j]])
                    else:
                        P.dma("sp", rec_d[j][:, OKT:RECW], r[:], reads=[bREC[ri]], writes=[bRECD[j]])

                pend_ = {}

                def p1_steps(xsrc, lite, i0, i1):
                    if i0 == 0:
                        pend_.clear()
                        p1_load(xsrc, 0)
                        p1_load(xsrc, 1)
                        pend_[0] = norm_tile(xst[0][:], [bxst[0]], MOD, bMOD)
                    for i in range(i0, i1):
                        if i + 2 < NT:
                            p1_load(xsrc, i + 2)
                        if i + 1 < NT:
                            pend_[i + 1] = norm_tile(xst[(i + 1) % 2][:], [bxst[(i + 1) % 2]], MOD, bMOD)
                        t = i - 4
                        if 0 <= t < NT:
                            if t % 4 == 0 and not lite:
                                p1_qk(t // 4)
                            p1_proj(t, lite)
                        if 0 <= t - 1 < NT:
                            p1_store(t - 1, lite)
                        if i < NT:
                            xn_, bxn_ = pend_.pop(i)
                            xT, bxT = xnT[(i // 4) % 2], bxnT[(i // 4) % 2]
                            transpose_tile(xn_, bxn_, lambda tl=i % 4, xT=xT: xT[:, :, tl * 128:(tl + 1) * 128], bxT, 7)

                p1_steps(xoth_d[0], True, 0, 4)
                for k in range(3):
                    p1_steps(xoth_d[k], True, 4, NT + 5)
                    if k + 1 < 3:
                        p1_steps(xoth_d[k + 1], True, 0, 4)
                    else:
                        p1_steps(x_d, False, 0, 4)
                    gates_pass(NT)
                    b1_lite(k)
                    for d in range(2):
                        P.op("dve", lambda e, k=k, d=d: e.tensor_copy(out=SALL[:, k, 516 * d:516 * (d + 1)], in_=ST[:, 0].rearrange("p a c -> p (a c)")),
                             reads=[bST], writes=[bSALL])
                    P.op("dve", lambda e, k=k: e.tensor_copy(out=SALL[:, k, 1032:1040], in_=FSEG[:].rearrange("p d a -> p (d a)")), reads=[bG], writes=[bSALL])
                p1_steps(x_d, False, 4, NT + 5)
                P.barrier_all()

            gates_pass(NTA)
            CS = sb0("CS", [128, NT, 2, 4, DVA], BF16)
            modc_bf = MODC[:].rearrange("p a n -> p (a n)").bitcast(BF16)
            for i_ in range(2, 4):
                WVt.append(modc_bf[:, (i_ - 2) * H * DVA:(i_ - 1) * H * DVA].rearrange("p (h c) -> p h c", c=DVA)); bWV.append(Buf("WV%d" % i_))
            P.op("dve", lambda e: e.memset(ST[:], 0.0), writes=[bST])
            ckv = lambda ct: (CK[:, ct, :], CV[:, ct])
            for stp in range(2):
                cf, cb = stp, 1 - stp
                state_step(ckv(cf)[0], ckv(cf)[1], bCKV, NT + cf, ckv(cb)[0], ckv(cb)[1], bCKV, NT + cb)
            P.op("dve", lambda e: e.tensor_copy(out=STC[:], in_=ST[:]), reads=[bST], writes=[bSTC])
            P.op("dve", lambda e: e.memset(ST[:], 0.0), reads=[bSTC], writes=[bST])
            b1_pass(CS)

            cms = contextlib.ExitStack()

            def sbc(name, shape, dt=F32):
                return cms.enter_context(nc.sbuf_tensor(un(name), list(shape), dt))
            fe = sbc("fe", [128, 4]); bfe = Buf("fe")
            for d in range(2):
                for k in (range(3) if d == 0 else range(2, -1, -1)):
                    fsrc = SALL[:, k, 1032 + 4 * d:1032 + 4 * d + 4]
                    dsrc = SALL[:, k, 516 * d:516 * (d + 1)]
                    P.op("dve", lambda e, fsrc=fsrc, k=k, d=d: e.tensor_scalar(
                        out=fe[:], in0=fsrc, scalar1=MK[:, k, 2 * d:2 * d + 1], scalar2=MK[:, k, 2 * d + 1:2 * d + 2],
                        op0=ALU.mult, op1=ALU.add), reads=[bSALL, bMK], writes=[bfe])
                    P.op("dve", lambda e, d=d: e.tensor_tensor(
                        out=STC[:, d], in0=STC[:, d], in1=fe[:].unsqueeze(2).to_broadcast([128, 4, DVA]), op=ALU.mult),
                        reads=[bSTC, bfe], writes=[bSTC])
                    P.op("dve", lambda e, d=d, dsrc=dsrc, k=k: e.scalar_tensor_tensor(
                        out=STC[:, d].rearrange("p a c -> p (a c)"), in0=dsrc, scalar=MK[:, k, 2 * d:2 * d + 1],
                        in1=STC[:, d].rearrange("p a c -> p (a c)"), op0=ALU.mult, op1=ALU.add),
                        reads=[bSALL, bMK, bSTC], writes=[bSTC])
            tmpS = sbc("tmpS", [128, 2, 4, DVA]); btmpS = Buf("tmpS")
            for j in range(NT):
                P.op("dve", lambda e, j=j: e.tensor_tensor(
                    out=tmpS[:].rearrange("p d a c -> p (d a) c"), in0=STC[:].rearrange("p d a c -> p (d a) c"),
                    in1=PFS[:, j].rearrange("p d a -> p (d a)").unsqueeze(2).to_broadcast([128, 8, DVA]), op=ALU.mult),
                    reads=[bSTC, bG], writes=[btmpS])
                P.op("dve", lambda e, j=j: e.tensor_tensor(
                    out=CS[:, j].rearrange("p d a c -> p (d a c)"), in0=CS[:, j].rearrange("p d a c -> p (d a c)"),
                    in1=tmpS[:].rearrange("p d a c -> p (d a c)"), op=ALU.add), reads=[btmpS, bCS[j]], writes=[bCS[j]])

            P.barrier_all()
            cms.close()
            xsb = [sb0("xsb%d" % i, [128, D]) for i in range(2)]; bxsb = [Buf("xsb0"), Buf("xsb1")]
            negm = sb0("negm", [128, 2, 8, 128], BF16); bNEG = Buf("negm")
            P.dma("pool", negm[:], negm_d[:, :, :, :], writes=[bNEG])
            WO = sb0("WO", [128, 8, D], BF16); bWO = Buf("WO")
            P.dma("pool", WO[:], wout_d.rearrange("(kc p) n -> p kc n", p=128), writes=[bWO])
            RB = [sb0("RB%d" % i, [128, RECW], BF16) for i in range(2)]; bRB = [Buf("RB0"), Buf("RB1")]
            TM0 = sb0("TM", [128, H, 128]); EB0 = sb0("EB", [128, H, 128])
            TMs = [TM0, nzs[0][:].rearrange("p (h t) -> p h t", t=128)]; bTMs = [Buf("TM0"), Buf("TM1")]
            EBs = [EB0, nzs[1][:].rearrange("p (h t) -> p h t", t=128)]; bEBs = [Buf("EB0"), Buf("EB1")]
            DT_ = sb0("DT", [128, 2, H, 128], BF16); bDT = [Buf("DT0"), Buf("DT1")]
            sall_bf = SALL[:].rearrange("p k w -> p (k w)").bitcast(BF16)
            PT0 = sb0("PT", [128, 2, H, 128], BF16)
            QS0 = sb0("QS", [128, 2, 4, 128], BF16)
            PT1 = sall_bf[:, 0:2048].rearrange("p (d h t) -> p d h t", d=2, h=H)
            QS1 = sall_bf[:, 2048:3072].rearrange("p (d a t) -> p d a t", d=2, a=4)
            PTs = [PT0, PT1]; QSs = [QS0, QS1]
            bPTs = [[Buf("PT%d_%d" % (i, d)) for d in range(2)] for i in range(2)]
            bQSs = [[Buf("QS%d_%d" % (i, d)) for d in range(2)] for i in range(2)]
            HX = MODC[:, 0, :].rearrange("p (h c) -> p h c", c=128); bHX = Buf("HX")
            HT = MODC[:, 1, :].rearrange("p (h c) -> p h c", c=128); bHT = Buf("HT")
            dn = sb0("dn", [128, 4, H]); bdn = Buf("dn")
            yb = sb0("yb", [128, D], BF16); byb = Buf("yb")
            yT = sb0("yT", [128, 8, 128], BF16); byT = Buf("yT")

            def load_rec(j):
                P.dma("sp", RB[j % 2][:], rec_d[j], reads=[bRECD[j], bRECQ[j]], writes=[bRB[j % 2]])

            def rec_views(j):
                R = RB[j % 2]
                return (R[:, OQ:OQ + 512].rearrange("p (a t) -> p a t", a=4), R[:, OK_:OK_ + 512].rearrange("p (a t) -> p a t", a=4),
                        R[:, OV:OV + H * DVA].rearrange("p (h c) -> p h c", c=DVA), R[:, OSO:OSO + D], bRB[j % 2])

            def a_z(j, d):
                TM, bTM = TMs[d], bTMs[d]
                tri = utri if d == 0 else ltri
                P.op("dve", lambda e: e.tensor_tensor(
                    out=TM[:].rearrange("p (l a) t -> p l a t", l=2),
                    in0=tri[:].unsqueeze(1).unsqueeze(1).to_broadcast([128, 2, 4, 128]),
                    in1=LF[:, j, d, :].rearrange("p (a l) -> p l a", l=2).unsqueeze(3).to_broadcast([128, 2, 4, 128]),
                    op=ALU.mult), reads=[bC, bG], writes=[bTM])

            def a_brow(j, d):
                TM, bTM = TMs[d], bTMs[d]
                for bank in range(2):
                    P.mm(lambda e, bank=bank: e.matmul(
                        PS[:, bank, :], lhsT=ones[:], rhs=TM[:, 4 * bank:4 * bank + 4, :], start=True, stop=True),
                        reads=[bC, bTM], writes=[PB[bank]], last=(bank == 1))

            def a_elem(j, d):
                TM, bTM, EB, bEB = TMs[d], bTMs[d], EBs[d], bEBs[d]
                qT, kT, va, so, bR = rec_views(j)
                QS, bQS = QSs[j % 2], bQSs[j % 2]
                brow = PS[:, 0:2, :].rearrange("p b (h t) -> p (b h) t", t=128)
                P.op("act", lambda e: e.activation(out=EB[:], in_=brow, func=AF.Exp), reads=[PB[0], PB[1]], writes=[bEB])
                P.op("dve", lambda e: e.tensor_tensor(out=TM[:], in0=brow, in1=negm[:, d], op=ALU.add),
                     reads=[PB[0], PB[1], bNEG], writes=[bTM])
                for h in range(H):
                    sl = (h % 2) * 4 + h // 2
                    P.op("act", lambda e, h=h, sl=sl: e.activation(
                        out=DT_[:, d, sl, :], in_=TM[:, sl, :], func=AF.Exp, bias=CVc[:, j, d, h:h + 1]),
                        reads=[bTM, bG], writes=[bDT[d]])
                EBv = EB[:].rearrange("p (l a) t -> p l a t", l=2)
                for hl in range(2):
                    P.op("dve", lambda e, hl=hl: e.tensor_tensor(
                        out=QS[hl * 64:(hl + 1) * 64, d], in0=qT[hl * 64:(hl + 1) * 64],
                        in1=EBv[hl * 64:(hl + 1) * 64, hl, :, :], op=ALU.mult),
                        reads=[bR, bEB], writes=[bQS[d]])

            def a_s(j):
                qT, kT, va, so, bR = rec_views(j)
                PT_, bPT = PTs[j % 2], bPTs[j % 2]
                for hl in range(2):
                    for hp in range(4):
                        bank = 2 + hl
                        P.mm(lambda e, hp=hp, hl=hl, bank=bank: e.matmul(
                            PS[:, bank, hp * 128:(hp + 1) * 128], lhsT=kT[hl * 64:(hl + 1) * 64, hp, :],
                            rhs=qT[hl * 64:(hl + 1) * 64, hp, :], start=True, stop=True),
                            reads=[bR], writes=[PB[bank]], last=(hp == 3))
                srow = PS[:, 2:4, :].rearrange("p b (h t) -> p (b h) t", t=128)
                for d in range(2):
                    P.op("dve", lambda e, d=d: e.tensor_tensor(out=PT_[:, d], in0=srow, in1=DT_[:, d], op=ALU.mult),
                         reads=[PB[2], PB[3], bDT[d]], writes=[bPT[d]])

            Hv = PS[:, 4:8, 0:2 * DVA].rearrange("p b (a c) -> p b a c", c=DVA)
            hb = [PB[4], PB[5], PB[6], PB[7]]

            def b_h(j, d):
                qT, kT, va, so, bR = rec_views(j)
                PT_, QS, bPT, bQS = PTs[j % 2], QSs[j % 2], bPTs[j % 2], bQSs[j % 2]
                for hl in range(2):
                    for hp in range(4):
                        h = 2 * hp + hl
                        sl = hl * 4 + hp
                        bank = 4 + sl // 2
                        c0 = (sl % 2) * DVA
                        P.mm(lambda e, h=h, sl=sl, bank=bank, c0=c0: e.matmul(
                            PS[:, bank, c0:c0 + DVA], lhsT=PT_[:, d, sl, :], rhs=va[:, h, :], start=True, stop=False),
                            reads=[bPT[d], bR], writes=[PB[bank]], last=False)
                        P.mm(lambda e, hp=hp, hl=hl, bank=bank, c0=c0: e.matmul(
                            PS[:, bank, c0:c0 + DVA], lhsT=QS[hl * 64:(hl + 1) * 64, d, hp, :],
                            rhs=CS[hl * 64:(hl + 1) * 64, j, d, hp, :], start=False, stop=True),
                            reads=[bQS[d], bCS[j]], writes=[PB[bank]], last=(sl % 2 == 1))

            def b_evac(j, d):
                dnv = dn[:, d].rearrange("p (b a) -> p b a", a=2)
                P.op("dve", lambda e: e.tensor_copy(out=dnv, in_=Hv[:, :, :, 128]), reads=hb, writes=[bdn])
                P.op("dve", lambda e: e.scalar_tensor_tensor(out=dn[:, 2 + d], in0=dn[:, d], scalar=-1.0, in1=dn[:, d],
                                                             op0=ALU.mult, op1=ALU.max), reads=[bdn], writes=[bdn])
                P.op("dve", lambda e: e.tensor_scalar(out=dn[:, 2 + d], in0=dn[:, 2 + d], scalar1=1.0, scalar2=None, op0=ALU.max),
                     reads=[bdn], writes=[bdn])
                P.op("dve", lambda e: e.reciprocal(out=dn[:, 2 + d], in_=dn[:, 2 + d]), reads=[bdn], writes=[bdn])
                tgt, btg = (HX, bHX) if d == 0 else (HT, bHT)
                for sl in range(H):
                    P.op("act", lambda e, sl=sl: e.activation(
                        out=tgt[:, sl, :], in_=Hv[:, sl // 2, sl % 2, 0:128], func=AF.Copy, scale=dn[:, 2 + d, sl:sl + 1]),
                        reads=[PB[4 + sl // 2], bdn], writes=[btg])

            def b_tail(j):
                qT, kT, va, so, bR = rec_views(j)
                P.op("dve", lambda e: e.tensor_tensor(out=HX[:], in0=HX[:], in1=HT[:], op=ALU.add), reads=[bHX, bHT], writes=[bHX])
                P.op("act", lambda e: e.activation(out=HT[:], in_=HX[:], func=AF.Square), reads=[bHX], writes=[bHT])
                P.op("dve", lambda e: e.tensor_reduce(out=dn[:, 0], in_=HT[:], axis=AX.X, op=ALU.add), reads=[bHT], writes=[bdn])
                P.op("dve", lambda e: e.tensor_scalar(out=dn[:, 1], in0=dn[:, 0], scalar1=1.0 / DV, scalar2=EPS, op0=ALU.mult, op1=ALU.add),
                     reads=[bdn], writes=[bdn])
                P.op("act", lambda e: e.activation(out=dn[:, 1], in_=dn[:, 1], func=AF.Ln), reads=[bdn], writes=[bdn])
                P.op("act", lambda e: e.activation(out=dn[:, 2], in_=dn[:, 1], func=AF.Exp, scale=-0.5), reads=[bdn], writes=[bdn])
                P.op("dve", lambda e: e.tensor_tensor(out=HT[:], in0=HX[:], in1=dn[:, 2].unsqueeze(2).to_broadcast([128, H, 128]), op=ALU.mult),
                     reads=[bHX, bdn], writes=[bHT])
                P.op("dve", lambda e: e.tensor_tensor(
                    out=yb[:].rearrange("p (a l c) -> p l a c", l=2, c=128), in0=HT[:].rearrange("p (l a) c -> p l a c", l=2),
                    in1=so.rearrange("p (a l c) -> p l a c", l=2, c=128), op=ALU.mult),
                    reads=[bHT, bR], writes=[byb])
                transpose_tile(yb, byb, lambda: yT[:], byT, 4)
                for nh in range(2):
                    for kc in range(8):
                        P.mm(lambda e, nh=nh, kc=kc: e.matmul(PS[:, 5 + nh, :], lhsT=yT[:, kc, :], rhs=WO[:, kc, nh * 512:(nh + 1) * 512],
                                                              start=(kc == 0), stop=(kc == 7)),
                             reads=[byT, bWO], writes=[PB[5 + nh]], last=(kc == 7))
                P.op("dve", lambda e: e.tensor_tensor(out=zt[:].rearrange("p (b c) -> p b c", b=2), in0=PS[:, 5:7, :],
                                                      in1=MOD[:, 2, :].rearrange("p (b c) -> p b c", b=2), op=ALU.mult),
                     reads=[PB[5], PB[6], bMOD[2]], writes=[bzt])
                xs_ = xsb[j % 2]
                P.dma("sp", xs_[:], x_d[j * 128:(j + 1) * 128, :], writes=[bxsb[j % 2]])
                P.op("dve", lambda e: e.tensor_tensor(out=xs_[:], in0=xs_[:], in1=zt[:], op=ALU.add),
                     reads=[bzt, bxsb[j % 2]], writes=[bxsb[j % 2]])
                P.dma("sp", xs_d[j], xs_[:], reads=[bxsb[j % 2]], writes=[bXS[j]])

            load_rec(0)
            load_rec(1)
            for d in range(2):
                a_z(0, d); a_brow(0, d); a_elem(0, d)
            a_s(0)
            for j in range(NT):
                nx = j + 1 < NT
                if nx:
                    a_z(j + 1, 0)
                b_h(j, 0)
                if nx:
                    a_brow(j + 1, 0)
                b_evac(j, 0)
                if nx:
                    a_elem(j + 1, 0)
                    a_z(j + 1, 1)
                b_h(j, 1)
                if nx:
                    a_brow(j + 1, 1)
                b_evac(j, 1)
                if nx:
                    a_elem(j + 1, 1)
                    a_s(j + 1)
                b_tail(j)
                if j + 2 < NT:
                    load_rec(j + 2)
            P.barrier_all()

        X = sb("X", [128, NT, D])
        for j in range(NT):
            P.dma("sp", X[:, j, :], xs_d[j], reads=[bXS[j]], writes=[bX[j]])

        def mlp(layer):
            with contextlib.ExitStack() as stk:
                def sbm(name, shape, dt=F32):
                    return stk.enter_context(nc.sbuf_tensor(un(name), list(shape), dt))
                with contextlib.ExitStack() as stk2:
                    compute_mods(layer, [3, 4, 5], MOD, bMOD, 0, stk2)
                    P.barrier_all()
                make_GS(n2g_d[layer:layer + 1, :])
                UT = sbm("UT", [128, 8, TOK], BF16); bUT = [Buf("UT%d" % i) for i in range(4)]
                bzth = [Buf("zth0"), Buf("zth1")]
                bXh = [[Buf("Xh%d_%d" % (i, k)) for k in range(2)] for i in range(NT)]
                P.barrier_all()
                W1s = [sbm("W1h%d" % i, [128, 8, 1024], BF16) for i in range(2)]; bW1s = [Buf("W1h0"), Buf("W1h1")]
                W2h = sbm("W2h", [128, 8, D], BF16); bW2 = Buf("W2h")
                H1T = sbm("H1T", [128, 8, 512], BF16); bH1 = Buf("H1T")
                rl = [sbm("rl%d" % i, [128, 512]) for i in range(2)]; brl = [Buf("rl0"), Buf("rl1")]
                w1v = w1_d[layer].rearrange("(kc p) n -> p kc n", p=128)
                w2v = w2_d[layer].rearrange("(hc p) n -> p hc n", p=128)

                def load_w1(hh):
                    for q in range(2):
                        c0 = hh * 1024 + q * 512
                        P.dma("pool", W1s[hh % 2][:, :, q * 512:(q + 1) * 512], w1v[:, :, c0:c0 + 512], writes=[bW1s[hh % 2]])

                def load_w2(hh):
                    P.dma("pool", W2h[:], w2v[:, hh * 8:(hh + 1) * 8, :], writes=[bW2])

                load_w1(0); load_w2(0); load_w1(1)
                pend = {}

                def nrm(j):
                    pend[j] = norm_tile(X[:, j, :], [bX[j]], MOD, bMOD)

                def trp(j):
                    xn_, bxn_ = pend.pop(j)
                    transpose_tile(xn_, bxn_, lambda j=j: UT[:, :, j * 128:(j + 1) * 128], bUT[j // 4], 7)

                nrm(0); nrm(1); trp(0); nrm(2); trp(1); nrm(3); trp(2); trp(3)
                n = 0
                for hh in range(4):
                    W1h, bW1 = W1s[hh % 2], bW1s[hh % 2]
                    if hh >= 1:
                        load_w2(hh)
                        if hh + 1 < 4:
                            load_w1(hh + 1)
                    for tg in range(4):
                        for hc in range(8):
                            if hh == 0 and tg + 1 < 4:
                                T = 4 * (tg + 1)
                                if hc >= 2 and hc - 2 < 4:
                                    trp(T + hc - 2)
                                if hc < 4:
                                    nrm(T + hc)
                            bank = 4 + (hc % 3 if hh == 0 else hc % 4)
                            for kc in range(8):
                                P.mm(lambda e, hc=hc, kc=kc, tg=tg, bank=bank, W1h=W1h: e.matmul(
                                    PS[:, bank, :], lhsT=W1h[:, kc, hc * 128:(hc + 1) * 128], rhs=UT[:, kc, tg * 512:(tg + 1) * 512],
                                    start=(kc == 0), stop=(kc == 7)), reads=[bW1, bUT[tg]], writes=[PB[bank]], last=(kc == 7))
                            s_ = n % 2; n += 1
                            P.op("act", lambda e, bank=bank, s_=s_: e.activation(out=rl[s_][:], in_=PS[:, bank, :], func=AF.Relu),
                                 reads=[PB[bank]], writes=[brl[s_]])
                            P.op("dve", lambda e, hc=hc, s_=s_: e.tensor_tensor(
                                out=H1T[:, hc, :], in0=rl[s_][:], in1=rl[s_][:], op=ALU.mult), reads=[brl[s_]], writes=[bH1])
                        for tl in range(4):
                            j = tg * 4 + tl
                            for nh in range(2):
                                bank = (tl * 2 + nh) % 4
                                for hc in range(8):
                                    P.mm(lambda e, hc=hc, tl=tl, nh=nh, bank=bank: e.matmul(
                                        PS[:, bank, :], lhsT=H1T[:, hc, tl * 128:(tl + 1) * 128], rhs=W2h[:, hc, nh * 512:(nh + 1) * 512],
                                        start=(hc == 0), stop=(hc == 7)), reads=[bH1, bW2], writes=[PB[bank]], last=(hc == 7))
                                P.op("dve", lambda e, nh=nh, bank=bank: e.tensor_tensor(
                                    out=zt[:, nh * 512:(nh + 1) * 512], in0=PS[:, bank, :], in1=MOD[:, 2, nh * 512:(nh + 1) * 512], op=ALU.mult),
                                    reads=[PB[bank], bMOD[2]], writes=[bzth[nh]])
                                P.op("dve", lambda e, j=j, nh=nh: e.tensor_tensor(
                                    out=X[:, j, nh * 512:(nh + 1) * 512], in0=X[:, j, nh * 512:(nh + 1) * 512],
                                    in1=zt[:, nh * 512:(nh + 1) * 512], op=ALU.add), reads=[bzth[nh], bXh[j][nh]], writes=[bXh[j][nh]])
                P.barrier_all()

        def finish(final_norm):
            evs = []
            if final_norm:
                load_vrep(fing_d[0:1, :])
            ob = [sb("ob%d" % i, [128, D]) for i in range(2)]; bob = [Buf("ob0"), Buf("ob1")]
            for j in range(NT):
                s_ = j % 2
                if final_norm:
                    P.op("act", lambda e, j=j: e.activation(out=nzs[0][:], in_=X[:, j, :], func=AF.Square, accum_out=stat[:, 0:1]),
                         reads=[bX[j]], writes=[bnzs[0], bstat])
                    P.op("act", lambda e: e.activation(out=stat[:, 1:2], in_=stat[:, 0:1], func=AF.Sqrt, scale=1.0 / D, bias=EPS),
                         reads=[bstat], writes=[bstat])
                    P.op("dve", lambda e: e.reciprocal(out=stat[:, 2:3], in_=stat[:, 1:2]), reads=[bstat], writes=[bstat])
                    P.op("dve", lambda e, j=j, s_=s_: e.scalar_tensor_tensor(out=ob[s_][:], in0=X[:, j, :], scalar=stat[:, 2:3], in1=vrep[:],
                                                                             op0=ALU.mult, op1=ALU.mult),
                         reads=[bX[j], bstat, bvrep], writes=[bob[s_]])
                    evs.append(P.dma("sp", out_d[j * 128:(j + 1) * 128, :], ob[s_][:], reads=[bob[s_]]))
                else:
                    evs.append(P.dma("sp", out_d[j * 128:(j + 1) * 128, :], X[:, j, :], reads=[bX[j]]))
            for ev in evs:
                P.wait_event("sp", ev)

        if stop in ("mix0", "comb", "b2a"):
            finish(False); return nc
        mlp(0)
        if stop == "l0":
            finish(False); return nc

        with contextlib.ExitStack() as l1:
            def sb1(name, shape, dt=F32):
                return l1.enter_context(nc.sbuf_tensor(un(name), list(shape), dt))
            with contextlib.ExitStack() as stk:
                compute_mods(1, [0, 1, 2], MOD, bMOD, 0, stk)
                P.barrier_all()
            make_GS(n1g_d[1:2, :])
            PMT = sb1("PMT", [128, 4, 128], BF16); PW = sb1("PW", [128, 4, 2, 256], BF16); bPC = Buf("poolc")
            P.dma("pool", PMT[:], pmt_d[:, :, :], writes=[bPC])
            P.dma("pool", PW[:], pw_d.rearrange("g (cc p) d -> p g cc d", p=128), writes=[bPC])
            GP = sb1("GP", [128, D]); bGP = Buf("GP")
            load_vrep(psc_d[0:1, :])
            P.op("dve", lambda e: e.tensor_tensor(out=GP[:], in0=vrep[:], in1=MOD[:, 2, :], op=ALU.mult), reads=[bvrep, bMOD[2]], writes=[bGP])
            PTbs = [sb1("PTb%d" % i, [128, 8, 128], BF16) for i in range(2)]; bPTbs = [Buf("PTb0"), Buf("PTb1")]
            nxt_ = norm_tile(X[:, 0, :], [bX[0]], MOD, bMOD)
            for j in range(NT):
                xnb, bxnb = nxt_
                if j + 1 < NT:
                    nxt_ = norm_tile(X[:, j + 1, :], [bX[j + 1]], MOD, bMOD)
                PTb, bPTb = PTbs[j % 2], bPTbs[j % 2]
                for gc in range(8):
                    g = gc // 2
                    bank = gc // 4
                    P.mm(lambda e, gc=gc, g=g, bank=bank: e.matmul(
                        PS[:, bank, (gc % 4) * 128:(gc % 4 + 1) * 128], lhsT=xnb[:, gc * 128:(gc + 1) * 128], rhs=PMT[:, g, :],
                        start=True, stop=True), reads=[bxnb, bPC], writes=[PB[bank]], last=(gc % 4 == 3))
                P.op("act", lambda e, PTb=PTb: e.activation(out=PTb[:], in_=PS[:, 0:2, :].rearrange("p b (a t) -> p (b a) t", t=128), func=AF.Copy),
                     reads=[PB[0], PB[1]], writes=[bPTb])
                for g in range(4):
                    bank = 2 + g // 2
                    for cc in range(2):
                        P.mm(lambda e, g=g, cc=cc, bank=bank, PTb=PTb: e.matmul(
                            PS[:, bank, (g % 2) * 256:(g % 2 + 1) * 256], lhsT=PTb[:, g * 2 + cc, :], rhs=PW[:, g, cc, :],
                            start=(cc == 0), stop=(cc == 1)), reads=[bPTb, bPC], writes=[PB[bank]], last=(cc == 1))
                P.op("dve", lambda e: e.tensor_tensor(out=zt[:].rearrange("p (b c) -> p b c", b=2), in0=PS[:, 2:4, :],
                                                      in1=GP[:].rearrange("p (b c) -> p b c", b=2), op=ALU.mult),
                     reads=[PB[2], PB[3], bGP], writes=[bzt])
                P.op("dve", lambda e, j=j: e.tensor_tensor(out=X[:, j, :], in0=X[:, j, :], in1=zt[:], op=ALU.add),
                     reads=[bzt, bX[j]], writes=[bX[j]])
            P.barrier_all()
        if stop == "mix1":
            finish(False); return nc
        mlp(1)
        finish(True)
    return nc


def _core_inputs(i, inp, consts):
    b, s = i // 4, i % 4
    f = lambda a: np.ascontiguousarray(np.asarray(a, dtype=np.float32))
    m = {}
    m["x"] = f(inp["x"][b, s * TOK:(s + 1) * TOK])
    m["ctx"] = f(inp["ctx"][b])
    cT = np.stack([np.asarray(inp["c"][b]).reshape(8, 128).T, np.asarray(inp["c_ctx"]).reshape(8, 128).T], -1)
    m["cT"] = f(cT)
    m["ada_w"] = f(inp["ada_w"]); m["ada_b"] = f(inp["ada_b"])
    m["norm1_g"] = f(inp["norm1_g"]); m["norm2_g"] = f(inp["norm2_g"])
    m["final_g"] = f(np.asarray(inp["final_g"]).reshape(1, D))
    m["mlstm_norm_g"] = f(np.asarray(inp["mlstm_norm_g"]).reshape(1, D))
    m["pool_scale"] = f(np.asarray(inp["pool_scale"]).reshape(1, D))
    m["gate_b"] = f(np.asarray(inp["mlstm_gate_b"]).reshape(1, 32))
    m["w_in"] = f(inp["mlstm_w_in"][0]); m["w_out"] = f(inp["mlstm_w_out"][0])
    m["pool_w"] = f(inp["pool_w"][0])
    m["w1"] = f(inp["mlp_w1"]); m["w2"] = f(inp["mlp_w2"])
    m.update(consts)
    mk = np.zeros((128, 3, 4), np.float32)
    xo = []
    for k in range(3):
        o = (s + k + 1) % 4
        mf = 1.0 if o < s else 0.0
        mb = 1.0 if o > s else 0.0
        mk[:, k] = [mf, 1.0 - mf, mb, 1.0 - mb]
        xo.append(np.asarray(inp["x"][b, o * TOK:(o + 1) * TOK], dtype=np.float32))
    m["segmask"] = mk
    m["x_oth"] = np.ascontiguousarray(np.stack(xo, 0))
    return m


_CACHE = {}


def _prog(mode, stop=None):
    key = (mode, stop)
    if key not in _CACHE:
        _CACHE[key] = build(mode, stop)
    return _CACHE[key]


def kernel(x, c, ctx, c_ctx, ada_w, ada_b, norm1_g, norm2_g, mlstm_w_in, mlstm_gate_b, mlstm_norm_g,
           mlstm_w_out, pool_w, pool_scale, mlp_w1, mlp_w2, final_g, _mode="fused", _stop=None):
    inp = dict(x=x, c=c, ctx=ctx, c_ctx=c_ctx, ada_w=ada_w, ada_b=ada_b, norm1_g=norm1_g, norm2_g=norm2_g,
               mlstm_w_in=mlstm_w_in, mlstm_gate_b=mlstm_gate_b, mlstm_norm_g=mlstm_norm_g, mlstm_w_out=mlstm_w_out,
               pool_w=pool_w, pool_scale=pool_scale, mlp_w1=mlp_w1, mlp_w2=mlp_w2, final_g=final_g)
    consts = _consts()
    maps = [_core_inputs(i, inp, consts) for i in range(8)]
    res = run_bass_kernel_spmd(_prog("fused", _stop), maps, core_ids=list(range(8)))
    out = np.zeros((2, 8192, D), np.float32)
    for i in range(8):
        out[i // 4, (i % 4) * TOK:(i % 4 + 1) * TOK] = np.asarray(res.results[i]["out"])
    return out
```
